# Optimizing a Trainium2 kernel written in Bass

```python
import math
import jax
import jax.numpy as jnp
from jax import lax
import numpy as np

D_MODEL = 1024
BATCH = 8
SEQ = 4096
DEPTH = 2

GRID_W = 64
CTX_LEN = 256
N_SUB = 3
FF_HIDDEN = 2816
LN_EPS = 1e-5
RMS_EPS = 1e-5
DEEPNORM_ALPHA = (2 * DEPTH) ** 0.25
DEEPNORM_BETA = (8 * DEPTH) ** -0.25
N_EVEN = (DEPTH + 1) // 2
N_ODD = DEPTH // 2

RW_HEADS = 8
RW_HEAD = 64
RW_W = RW_HEADS * RW_HEAD
RW_DECAY_LORA = 64
RW_A_LORA = 64
RW_GATE_LORA = 160
RW_GN_EPS = 64e-5
RW_SPLITS = (RW_W, 2 * RW_W, 3 * RW_W, 3 * RW_W + 2 * RW_DECAY_LORA,
             3 * RW_W + 2 * RW_DECAY_LORA + RW_A_LORA)
A_TOTAL = 3 * RW_W + 2 * RW_DECAY_LORA + RW_A_LORA + RW_GATE_LORA

HG_HEADS = 4
HG_DK = 128
HG_DV = 128
HG_KW = HG_HEADS * HG_DK
HG_VW = HG_HEADS * HG_DV
HG_CHUNK = 64
HG_SPLITS = (HG_KW, 3 * HG_KW, 3 * HG_KW + HG_VW)
B_TOTAL = 3 * HG_KW + 2 * HG_VW
EVEN_PROJ = A_TOTAL + B_TOTAL
EVEN_MIX = RW_W + HG_VW

DA_HEADS = 8
DA_HEAD = 64
DA_W = DA_HEADS * 2 * DA_HEAD
Q_BLOCK = 128
ROPE_BASE = 10000.0

kernel_name = 'hybrid_rwkv7_hgrn2_diffattn_prefix_dit'


def _layernorm(t, g, b):
    tf = t.astype(jnp.float32)
    mu = jnp.mean(tf, axis=-1, keepdims=True)
    var = jnp.mean(jnp.square(tf - mu), axis=-1, keepdims=True)
    return (tf - mu) * lax.rsqrt(var + LN_EPS) * g + b


def _rmsnorm(t, g, eps):
    tf = t.astype(jnp.float32)
    return tf * lax.rsqrt(jnp.mean(tf * tf, axis=-1, keepdims=True) + eps) * g


def _modulation(cvec, w, b):
    m = jax.nn.silu(cvec) @ w + b
    return m.reshape(m.shape[:-1] + (N_SUB, 3, D_MODEL))


def _modulate(h, m, j):
    return h * (1.0 + m[..., j, 1, :]) + m[..., j, 0, :]


def _post_norm(h, out, gate, weight, g, b):
    return _layernorm(DEEPNORM_ALPHA * h + weight * gate * out, g, b)


def _swiglu(h, w_in, w_out):
    gt, up = jnp.split(h @ w_in, 2, axis=-1)
    return (jax.nn.silu(gt) * up) @ w_out


def _ffn_sublayer(h, m, j, w_in, w_out, g, b):
    out = _swiglu(_modulate(h, m, j), w_in, w_out)
    return _post_norm(h, out, m[..., j, 2, :], 0.5, g, b)


def _centred_shift(u):
    p = jnp.pad(u, ((0, 0), (1, 1), (0, 0)))
    return 0.5 * (p[:, :-2] + p[:, 2:])


def _rwkv7_prep(u, mu, w0, w2, a0, a2, g2, k_k, k_a):
    B, T, _ = u.shape
    u = u.astype(jnp.float32)
    u = u + mu * (_centred_shift(u) - u)
    r, k, v, wd, ad, gd = jnp.split(u, RW_SPLITS, axis=-1)
    wd = wd.reshape(B, T, 2, RW_DECAY_LORA)
    w = -jax.nn.softplus(-(w0 + jnp.einsum('btdr,drc->btdc', jnp.tanh(wd), w2))) - 0.5
    decay = jnp.exp(-jnp.exp(w)).reshape(B, T, 2, RW_HEADS, RW_HEAD)
    a = jax.nn.sigmoid(a0 + ad @ a2)
    g = jax.nn.sigmoid(gd) @ g2
    kk = (k * k_k).reshape(B, T, RW_HEADS, RW_HEAD)
    kk = kk / jnp.maximum(jnp.sqrt(jnp.sum(kk * kk, axis=-1, keepdims=True)), 1e-12)
    k = k * (1.0 + (a - 1.0) * k_a)
    heads = lambda t: t.reshape(B, T, RW_HEADS, RW_HEAD)
    return dict(r=heads(r), k=heads(k), v=heads(v), kk=kk, a=heads(a), decay=decay, g=g)


def _rwkv7_scan(s0, p, d, reverse):
    seq = lambda t: jnp.moveaxis(t, 1, 0)
    xs = (seq(p['r']), seq(p['decay'][:, :, d]), seq(p['k']), seq(p['v']),
          seq(-p['kk']), seq(p['kk'] * p['a']))

    def step(S, inp):
        r_t, w_t, k_t, v_t, z_t, b_t = inp
        sa = jnp.einsum('bhvk,bhk->bhv', S, z_t)
        S = S * w_t[:, :, None, :] + sa[..., None] * b_t[:, :, None, :] + v_t[..., None] * k_t[:, :, None, :]
        return S, jnp.einsum('bhvk,bhk->bhv', S, r_t)

    S, y = lax.scan(step, s0, xs, reverse=reverse)
    return S, jnp.moveaxis(y, 0, 1)


def _rwkv7_readout(y, p, r_k, gn_g, gn_b):
    B, T, H, N = y.shape
    mu = jnp.mean(y, axis=-1, keepdims=True)
    var = jnp.mean(jnp.square(y - mu), axis=-1, keepdims=True)
    y = (y - mu) * lax.rsqrt(var + RW_GN_EPS) * gn_g.reshape(H, N) + gn_b.reshape(H, N)
    y = y + jnp.sum(p['r'] * p['k'] * r_k, axis=-1, keepdims=True) * p['v']
    return y.reshape(B, T, H * N) * p['g']


def _rwkv7_mixer(u_ctx, u_lat, mu, w0, w2, a0, a2, g2, k_k, k_a, r_k, gn_g, gn_b, with_ctx_out):
    pc = _rwkv7_prep(u_ctx, mu, w0, w2, a0, a2, g2, k_k, k_a)
    pl = _rwkv7_prep(u_lat, mu, w0, w2, a0, a2, g2, k_k, k_a)
    B = u_lat.shape[0]
    s0 = jnp.zeros((B, RW_HEADS, RW_HEAD, RW_HEAD), jnp.float32)
    y_lat, y_ctx = 0.0, 0.0
    for d in range(2):
        s_ctx, yc = _rwkv7_scan(s0, pc, d, d == 1)
        _, yl = _rwkv7_scan(s_ctx, pl, d, d == 1)
        y_lat = y_lat + yl
        if with_ctx_out:
            y_ctx = y_ctx + yc
    out_lat = _rwkv7_readout(y_lat, pl, r_k, gn_g, gn_b)
    out_ctx = _rwkv7_readout(y_ctx, pc, r_k, gn_g, gn_b) if with_ctx_out else None
    return out_ctx, out_lat


def _hgrn2_prep(u, lb):
    B, T, _ = u.shape
    u = u.astype(jnp.float32)
    q, f, i, g = jnp.split(u, HG_SPLITS, axis=-1)
    fg = lb + (1.0 - lb) * jax.nn.sigmoid(f.reshape(B, T, 2, HG_KW))
    fg = fg.reshape(B, T, 2, HG_HEADS, HG_DK)
    return dict(q=jax.nn.silu(q).reshape(B, T, HG_HEADS, HG_DK), logf=jnp.log(fg), k=1.0 - fg,
                i=i.reshape(B, T, HG_HEADS, HG_DV), g=g.reshape(B, T, HG_HEADS, HG_DV))


def _gla_chunked(q, k, v, logf, s0):
    B, T, H, K = q.shape
    n, L = T // HG_CHUNK, HG_CHUNK
    q, k, v, logf = [t.reshape(B, n, L, H, -1) for t in (q, k, v, logf)]
    b = jnp.cumsum(logf, axis=2)
    b_last = b[:, :, -1]
    qb = q * jnp.exp(b)
    kb = k * jnp.exp(-b)
    kd = k * jnp.exp(b_last[:, :, None] - b)
    scores = jnp.einsum('bnlhk,bnshk->bnhls', qb, kb)
    scores = jnp.where(jnp.tril(jnp.ones((L, L), bool)), scores, 0.0)
    o_intra = jnp.einsum('bnhls,bnshv->bnlhv', scores, v)

    def step(S, inp):
        qb_c, kd_c, v_c, gl_c = inp
        o = jnp.einsum('blhk,bhkv->blhv', qb_c, S)
        S = S * gl_c[..., None] + jnp.einsum('blhk,blhv->bhkv', kd_c, v_c)
        return S, o

    xs = (jnp.moveaxis(qb, 1, 0), jnp.moveaxis(kd, 1, 0), jnp.moveaxis(v, 1, 0),
          jnp.moveaxis(jnp.exp(b_last), 1, 0))
    S, o_inter = lax.scan(step, s0, xs)
    o = o_intra + jnp.moveaxis(o_inter, 0, 1)
    return S, o.reshape(B, T, H, -1)


def _hgrn2_readout(o, p, norm_g):
    B, T = o.shape[:2]
    return (_rmsnorm(o, norm_g, RMS_EPS) * jax.nn.silu(p['g'])).reshape(B, T, HG_VW)


def _hgrn2_mixer(u_ctx, u_lat, lb, norm_g, with_ctx_out):
    pc = _hgrn2_prep(u_ctx, lb)
    pl = _hgrn2_prep(u_lat, lb)
    B = u_lat.shape[0]
    s0 = jnp.zeros((B, HG_HEADS, HG_DK, HG_DV), jnp.float32)
    o_lat, o_ctx = 0.0, 0.0
    for d in range(2):
        fl = (lambda t: jnp.flip(t, axis=1)) if d == 1 else (lambda t: t)
        s_ctx, oc = _gla_chunked(fl(pc['q']), fl(pc['k'][:, :, d]), fl(pc['i']), fl(pc['logf'][:, :, d]), s0)
        _, ol = _gla_chunked(fl(pl['q']), fl(pl['k'][:, :, d]), fl(pl['i']), fl(pl['logf'][:, :, d]), s_ctx)
        o_lat = o_lat + fl(ol)
        if with_ctx_out:
            o_ctx = o_ctx + fl(oc)
    out_lat = _hgrn2_readout(o_lat, pl, norm_g)
    out_ctx = _hgrn2_readout(o_ctx, pc, norm_g) if with_ctx_out else None
    return out_ctx, out_lat


def _even_mixer(h_ctx, h_lat, w_in, w_out, rw, lb, hg_norm_g, with_ctx_out):
    u_ctx = h_ctx @ w_in
    u_lat = h_lat @ w_in
    ra_ctx, ra_lat = _rwkv7_mixer(u_ctx[..., :A_TOTAL], u_lat[..., :A_TOTAL], *rw, with_ctx_out)
    hb_ctx, hb_lat = _hgrn2_mixer(u_ctx[..., A_TOTAL:], u_lat[..., A_TOTAL:], lb, hg_norm_g, with_ctx_out)
    out_lat = jnp.concatenate([ra_lat, hb_lat], axis=-1) @ w_out
    out_ctx = jnp.concatenate([ra_ctx, hb_ctx], axis=-1) @ w_out if with_ctx_out else None
    return out_ctx, out_lat


def _axial_rope_tables(n_tokens):
    n_rows = n_tokens // GRID_W
    row = jnp.repeat(jnp.arange(n_rows), GRID_W).astype(jnp.float32)
    col = jnp.tile(jnp.arange(GRID_W), n_rows).astype(jnp.float32)
    n_freq = DA_HEAD // 4
    inv = ROPE_BASE ** (-jnp.arange(n_freq, dtype=jnp.float32) / n_freq)
    ar = row[:, None] * inv
    ac = col[:, None] * inv
    ang = jnp.concatenate([ar, ar, ac, ac], axis=-1)
    return jnp.cos(ang), jnp.sin(ang)


def _rotate_half(t):
    h = t.shape[-1] // 2
    return jnp.concatenate([-t[..., h:], t[..., :h]], axis=-1)


def _apply_axial_rope(t, cos, sin):
    h = DA_HEAD // 2
    rot = jnp.concatenate([_rotate_half(t[..., :h]), _rotate_half(t[..., h:])], axis=-1)
    return t * cos[None, :, None, None, :] + rot * sin[None, :, None, None, :]


def _diff_split(u):
    B, T, _ = u.shape
    q, k, v = jnp.split(u, 3, axis=-1)
    return (q.reshape(B, T, DA_HEADS, 2, DA_HEAD), k.reshape(B, T, DA_HEADS, 2, DA_HEAD),
            v.reshape(B, T, DA_HEADS, 2 * DA_HEAD))


def _diff_softmax(q, k, v, lam):
    s = jnp.einsum('bqhmd,bshmd->bhmqs', q, k).astype(jnp.float32) * (DA_HEAD ** -0.5)
    p = jax.nn.softmax(s, axis=-1)
    a = p[:, :, 0] - lam * p[:, :, 1]
    return jnp.einsum('bhqs,bshe->bqhe', a, v.astype(jnp.float32))


def _diff_readout(o, norm_g, lam_init, w_out):
    B, T = o.shape[:2]
    o = _rmsnorm(o, norm_g, RMS_EPS) * (1.0 - lam_init)
    return o.reshape(B, T, DA_W) @ w_out


def _diff_attn_mixer(h_ctx, h_lat, w_in, w_out, lam_vecs, norm_g, lam_init, cos, sin, with_ctx_out):
    B, T, _ = h_lat.shape
    qc, kc, vc = _diff_split(h_ctx @ w_in)
    ql, kl, vl = _diff_split(h_lat @ w_in)
    ql = _apply_axial_rope(ql, cos, sin)
    kl = _apply_axial_rope(kl, cos, sin)
    lv = lam_vecs.astype(jnp.float32)
    lam = jnp.exp(jnp.sum(lv[0] * lv[1])) - jnp.exp(jnp.sum(lv[2] * lv[3])) + lam_init
    k_all = jnp.concatenate([kc, kl], axis=1)
    v_all = jnp.concatenate([vc, vl], axis=1)
    n_blk = T // Q_BLOCK
    q_blocks = jnp.moveaxis(ql.reshape(B, n_blk, Q_BLOCK, DA_HEADS, 2, DA_HEAD), 1, 0)
    o_lat = lax.map(lambda qb: _diff_softmax(qb, k_all, v_all, lam), q_blocks)
    o_lat = jnp.moveaxis(o_lat, 0, 1).reshape(B, T, DA_HEADS, 2 * DA_HEAD)
    out_lat = _diff_readout(o_lat, norm_g, lam_init, w_out)
    out_ctx = _diff_readout(_diff_softmax(qc, kc, vc, lam), norm_g, lam_init, w_out) if with_ctx_out else None
    return out_ctx, out_lat


def setup_inputs(seed: int = 0) -> dict:
    key = jax.random.key(seed)
    ks = jax.random.split(key, 29)
    D = D_MODEL
    nrm = lambda i, shape, scale: scale * jax.random.normal(ks[i], shape, jnp.float32)
    return {
        'x': nrm(0, (BATCH, SEQ, D), 1.0),
        'c': nrm(1, (BATCH, D), 1.0),
        'ctx': nrm(2, (BATCH, CTX_LEN, D), 1.0),
        'c_ctx': nrm(3, (D,), 1.0),
        'w_mod': nrm(4, (DEPTH, D, N_SUB * 3 * D), 0.5 * D ** -0.5),
        'b_mod': nrm(5, (DEPTH, N_SUB * 3 * D), 0.02),
        'ln_g': 1.0 + nrm(6, (DEPTH, N_SUB, D), 0.02),
        'ln_b': nrm(7, (DEPTH, N_SUB, D), 0.02),
        'ffn_w_in': nrm(8, (DEPTH, 2, D, 2 * FF_HIDDEN), D ** -0.5),
        'ffn_w_out': nrm(9, (DEPTH, 2, FF_HIDDEN, D), DEEPNORM_BETA * FF_HIDDEN ** -0.5),
        'ev_w_in': nrm(10, (N_EVEN, D, EVEN_PROJ), D ** -0.5),
        'ev_w_out': nrm(11, (N_EVEN, EVEN_MIX, D), DEEPNORM_BETA * EVEN_MIX ** -0.5),
        'rw_mu': jax.random.uniform(ks[12], (N_EVEN, A_TOTAL), jnp.float32),
        'rw_w0': jnp.linspace(-6.0, -1.0, RW_W, dtype=jnp.float32) + nrm(13, (N_EVEN, 2, RW_W), 0.1),
        'rw_w2': nrm(14, (N_EVEN, 2, RW_DECAY_LORA, RW_W), 0.1 * RW_DECAY_LORA ** -0.5),
        'rw_a0': nrm(15, (N_EVEN, RW_W), 0.1),
        'rw_a2': nrm(16, (N_EVEN, RW_A_LORA, RW_W), 0.1 * RW_A_LORA ** -0.5),
        'rw_g2': nrm(17, (N_EVEN, RW_GATE_LORA, RW_W), RW_GATE_LORA ** -0.5),
        'rw_k_k': 0.85 + nrm(18, (N_EVEN, RW_W), 0.05),
        'rw_k_a': 1.0 + nrm(19, (N_EVEN, RW_W), 0.05),
        'rw_r_k': nrm(20, (N_EVEN, RW_HEADS, RW_HEAD), 0.1),
        'rw_gn_g': 1.0 + nrm(21, (N_EVEN, RW_W), 0.02),
        'rw_gn_b': nrm(22, (N_EVEN, RW_W), 0.02),
        'hg_lb': nrm(23, (DEPTH + 1, HG_KW), 0.1),
        'hg_norm_g': 1.0 + nrm(24, (N_EVEN, HG_DV), 0.02),
        'od_w_in': nrm(25, (N_ODD, D, 3 * DA_W), D ** -0.5),
        'od_w_out': nrm(26, (N_ODD, DA_W, D), DEEPNORM_BETA * DA_W ** -0.5),
        'da_lambda': nrm(27, (N_ODD, 4, DA_HEAD), 0.1),
        'da_norm_g': 1.0 + nrm(28, (N_ODD, 2 * DA_HEAD), 0.02),
    }


def reference(x, c, ctx, c_ctx, w_mod, b_mod, ln_g, ln_b, ffn_w_in, ffn_w_out, ev_w_in, ev_w_out,
              rw_mu, rw_w0, rw_w2, rw_a0, rw_a2, rw_g2, rw_k_k, rw_k_a, rw_r_k, rw_gn_g, rw_gn_b,
              hg_lb, hg_norm_g, od_w_in, od_w_out, da_lambda, da_norm_g):
    T = x.shape[1]
    cos, sin = _axial_rope_tables(T)
    lower_bounds = jnp.cumsum(jax.nn.softmax(hg_lb.astype(jnp.float32), axis=0), axis=0)
    h_lat, h_ctx = x, ctx
    for i in range(DEPTH):
        with_ctx_out = i < DEPTH - 1
        j = i // 2
        m_lat = _modulation(c, w_mod[i], b_mod[i])[:, None]
        m_ctx = _modulation(c_ctx, w_mod[i], b_mod[i])[None, None]
        h_lat = _ffn_sublayer(h_lat, m_lat, 0, ffn_w_in[i, 0], ffn_w_out[i, 0], ln_g[i, 0], ln_b[i, 0])
        h_ctx = _ffn_sublayer(h_ctx, m_ctx, 0, ffn_w_in[i, 0], ffn_w_out[i, 0], ln_g[i, 0], ln_b[i, 0])
        a_lat = _modulate(h_lat, m_lat, 1)
        a_ctx = _modulate(h_ctx, m_ctx, 1)
        if i % 2 == 0:
            rw = (rw_mu[j], rw_w0[j], rw_w2[j], rw_a0[j], rw_a2[j], rw_g2[j], rw_k_k[j], rw_k_a[j],
                  rw_r_k[j], rw_gn_g[j], rw_gn_b[j])
            o_ctx, o_lat = _even_mixer(a_ctx, a_lat, ev_w_in[j], ev_w_out[j], rw, lower_bounds[i],
                                       hg_norm_g[j], with_ctx_out)
        else:
            lam_init = 0.8 - 0.6 * math.exp(-0.3 * i)
            o_ctx, o_lat = _diff_attn_mixer(a_ctx, a_lat, od_w_in[j], od_w_out[j], da_lambda[j], da_norm_g[j],
                                            lam_init, cos, sin, with_ctx_out)
        h_lat = _post_norm(h_lat, o_lat, m_lat[..., 1, 2, :], 1.0, ln_g[i, 1], ln_b[i, 1])
        h_lat = _ffn_sublayer(h_lat, m_lat, 2, ffn_w_in[i, 1], ffn_w_out[i, 1], ln_g[i, 2], ln_b[i, 2])
        if with_ctx_out:
            h_ctx = _post_norm(h_ctx, o_ctx, m_ctx[..., 1, 2, :], 1.0, ln_g[i, 1], ln_b[i, 1])
            h_ctx = _ffn_sublayer(h_ctx, m_ctx, 2, ffn_w_in[i, 1], ffn_w_out[i, 1], ln_g[i, 2], ln_b[i, 2])
    return h_lat
```

```python
import math
from contextlib import ExitStack
import numpy as np
import concourse.bass as bass
import concourse.mybir as mybir
from concourse.bass_utils import run_bass_kernel_spmd

F32 = mybir.dt.float32
BF16 = mybir.dt.bfloat16
AF = mybir.ActivationFunctionType
ALU = mybir.AluOpType
AX = mybir.AxisListType

D = 1024
FF = 2816
DEPTH = 2
ALPHA = (2 * DEPTH) ** 0.25
LN_EPS = 1e-5


class Sched:
    ENG = ('pe', 'act', 'dve', 'pool', 'sp')

    def __init__(self, nc, stack, needed=None, n_dma_sems=(('sp', 24), ('act', 8), ('pool', 16))):
        self.nc = nc
        self.needed = needed
        self.sig = {e: 0 for e in self.ENG}
        self.sigmap = {}
        self.waited = set()
        self.eng = {'pe': nc.tensor, 'act': nc.scalar, 'dve': nc.vector, 'pool': nc.gpsimd, 'sp': nc.sync}
        self.sem, self.cnt, self.seen = {}, {}, {}
        for e in self.ENG:
            self.sem[e] = stack.enter_context(nc.semaphore('s_' + e))
            self.cnt[e] = 0
            self.seen[e] = {}
        self.dpool, self.dnext = {}, {}
        for q, n in n_dma_sems:
            lst = []
            for i in range(n):
                name = 'd_%s%d' % (q, i)
                self.sem[name] = stack.enter_context(nc.semaphore(name))
                self.cnt[name] = 0
                lst.append(name)
            self.dpool[q] = lst
            self.dnext[q] = 0
        self.snap, self.lastw, self.readers = {}, {}, {}
        self.ninstr = 0

    def _wait(self, e, chan, count):
        if chan == e and e == 'pe':
            return
        s = self.seen[e]
        if s.get(chan, 0) >= count:
            return
        if chan in self.sig:
            self.waited.add((chan, count))
            val = self.sigmap[(chan, count)] if self.needed is not None else count
        else:
            val = count
        self.eng[e].wait_ge(self.sem[chan], val)
        self.ninstr += 1
        s[chan] = count
        sn = self.snap.get((chan, count))
        if sn:
            for k, v in sn.items():
                if s.get(k, 0) < v:
                    s[k] = v

    def _deps(self, e, reads, writes):
        deps = {}
        for k in reads:
            w = self.lastw.get(k)
            if w is not None and deps.get(w[0], 0) < w[1]:
                deps[w[0]] = w[1]
        for k in writes:
            w = self.lastw.get(k)
            if w is not None and deps.get(w[0], 0) < w[1]:
                deps[w[0]] = w[1]
            for c, n in self.readers.get(k, {}).items():
                if c == e:
                    continue
                if deps.get(c, 0) < n:
                    deps[c] = n
        return deps

    def _record(self, chan, count, reads, writes):
        for k in reads:
            self.readers.setdefault(k, {})[chan] = count
        for k in writes:
            self.lastw[k] = (chan, count)
            self.readers[k] = {}

    def op(self, e, fn, reads=(), writes=()):
        for c, n in self._deps(e, reads, writes).items():
            self._wait(e, c, n)
        ins = fn(self.eng[e])
        self.cnt[e] += 1
        n = self.cnt[e]
        if self.needed is None or (e, n) in self.needed:
            self.sig[e] += 1
            self.sigmap[(e, n)] = self.sig[e]
            ins.then_inc(self.sem[e], 1)
        self.ninstr += 1
        self.snap[(e, n)] = dict(self.seen[e])
        self._record(e, n, reads, writes)
        return ins

    def dma(self, q, out, in_, reads=(), writes=(), **kw):
        lst = self.dpool[q]
        name = lst[self.dnext[q] % len(lst)]
        self.dnext[q] += 1
        deps = self._deps(q, reads, writes)
        if self.cnt[name] > 0:
            deps[name] = max(deps.get(name, 0), self.cnt[name])
        for c, n in deps.items():
            self._wait(q, c, n)
        ins = self.eng[q].dma_start(out=out, in_=in_, **kw)
        self.cnt[name] += 16
        n = self.cnt[name]
        ins.then_inc(self.sem[name], 16)
        self.ninstr += 1
        self.snap[(name, n)] = dict(self.seen[q])
        self._record(name, n, reads, writes)
        return ins

    def barrier(self):
        for e in self.ENG:
            for c in list(self.cnt.keys()):
                if self.cnt[c] > 0:
                    self._wait(e, c, self.cnt[c])

    def finish(self, e='sp'):
        for c in list(self.cnt.keys()):
            if self.cnt[c] > 0 and c != e:
                self._wait(e, c, self.cnt[c])


class Ctx:
    pass


_UID = [0]


def sbt(nc, ph, name, shape, dt):
    _UID[0] += 1
    return ph.enter_context(nc.sbuf_tensor('sb%d_%s' % (_UID[0], name), list(shape), dt))


def build(TL, TC, layers=(0, 1), debug=False, needed=None, full_out=False):
    NT = TL + TC
    nct, nlt = TC // 128, TL // 128
    ntile = nct + nlt
    nc = bass.Bass("TRN2", target_bir_lowering=False)
    K = Ctx()
    K.nc, K.TL, K.TC, K.NT, K.nct, K.nlt, K.ntile = nc, TL, TC, NT, nct, nlt, ntile

    def din(name, shape):
        return nc.dram_tensor(name, list(shape), F32, kind="ExternalInput").ap()

    def dscr(name, shape, dt=F32):
        kind = "ExternalOutput" if debug else "Internal"
        return nc.dram_tensor(name, list(shape), dt, kind=kind).ap()

    I = {}
    for name, shape in input_shapes(TL, TC).items():
        I[name] = din(name, shape)
    K.I = I
    out = nc.dram_tensor('out', [NT if full_out else TL, D], F32, kind="ExternalOutput").ap()
    K.Hs = [dscr('Hs0', [NT, D]), dscr('Hs1', [NT, D])]
    K.Mrow = dscr('Mrow', [2, 2, 9 * D])
    K.Upad = dscr('Upad', [NT + 2, 4448])
    K.U = K.Upad[1:NT + 1, :]
    K.Mix = dscr('Mix', [NT, D])
    K.Yf = dscr('Yf', [NT, D])
    K.QT = dscr('QT', [128, 8, TL], BF16)
    K.KT = dscr('KT', [128, 8, NT], BF16)
    QKV = K.U[:, 0:3072]
    alltiles = list(range(ntile))
    lat = list(range(nct, ntile))

    with ExitStack() as st:
        S = Sched(nc, st, needed=needed)
        K.S = S
        K.PS = [st.enter_context(nc.psum_tensor('B%d' % b, [128, 512], F32)) for b in range(8)]
        K.ident = sbt(nc, st, 'ident', [128, 128], F32)
        S.dma('sp', K.ident[:], I['ident'], writes=['ident'])

        phase_mod(K)
        src, skey = I['h0'], 'h0'
        cur = 1

        def nxt():
            nonlocal cur
            d = (K.Hs[cur], 'Hs%d' % cur)
            cur = 1 - cur
            return d

        if 0 in layers:
            dst, dkey = nxt()
            ffn_phase(K, 0, 0, 0, src, skey, dst, dkey, alltiles)
            src, skey = dst, dkey
            proj_phase(K, 0, I['ev_w_in'], 4448, src, skey, K.U, 'U', alltiles)
            even_phase(K, 0, K.U, 'U', K.Mix, 'Mix')
            dst, dkey = nxt()
            outproj_phase(K, 0, I['ev_w_out'], K.Mix, 'Mix', src, skey, dst, dkey, alltiles)
            src, skey = dst, dkey
            dst, dkey = nxt()
            ffn_phase(K, 0, 1, 2, src, skey, dst, dkey, alltiles)
            src, skey = dst, dkey
        if 1 in layers:
            dst, dkey = nxt()
            ffn_phase(K, 1, 0, 0, src, skey, dst, dkey, alltiles)
            src, skey = dst, dkey
            proj_phase(K, 1, I['od_w_in'], 3072, src, skey, QKV, 'U', alltiles)
            attn_phase(K, 1, QKV, 'U', K.Mix, 'Mix')
            dst, dkey = nxt()
            outproj_phase(K, 1, I['od_w_out'], K.Mix, 'Mix', src, skey, dst, dkey, lat)
            src, skey = dst, dkey
            dst, dkey = nxt()
            ffn_phase(K, 1, 1, 2, src, skey, dst, dkey, lat)
            src, skey = dst, dkey
        with ExitStack() as ph:
            S.barrier()
            ft = sbt(nc, ph, 'fin_t', [128, 2, D], F32)
            t0_ = 0 if full_out else nct
            for ti in range(ntile - t0_):
                b = ti % 2
                S.dma('sp', ft[:, b, :], src[(t0_ + ti) * 128:(t0_ + ti + 1) * 128, :],
                      reads=[skey], writes=['fin%d' % b])
                S.dma('sp', out[ti * 128:(ti + 1) * 128, :], ft[:, b, :], reads=['fin%d' % b], writes=['out'])
        S.finish('sp')
    K.ninstr = S.ninstr
    build.last = K
    return nc


def input_shapes(TL, TC):
    NT = TL + TC
    return {
        'h0': [NT, D], 'cc': [128, 8, 2], 'w_mod': [2, D, 9 * D], 'b_mod': [2, 9 * D],
        'ln_g': [2, 3, D], 'ln_b': [2, 3, D], 'ffn_w_in': [2, 2, D, 2 * FF], 'ffn_w_out': [2, 2, FF, D],
        'ident': [128, 128],
        'ev_w_in': [D, 4448], 'ev_w_out': [D, D],
        'od_w_in': [D, 3072], 'od_w_out': [D, D], 'da_lambda': [4, 64], 'da_norm_g': [128],
        'rope_cos': [TL, 64], 'rope_sin': [TL, 64],
        'masks': [7, 128, 128], 'hg_lb': [3, 512], 'hg_norm_g': [128],
        'rw_mu': [1888], 'rw_w0': [2, 512], 'rw_w2p': [2, 128, 512], 'rw_a0': [512], 'rw_a2p': [128, 512], 'rw_g2a': [128, 512], 'rw_g2b': [128, 512],
        'rowmask': [128, 4],
        'rw_k_k': [512], 'rw_k_a': [512], 'rw_r_k': [512], 'rw_gn_g': [512], 'rw_gn_b': [512],
    }


SKIP = set()


def zero_mix(K, Mix, mkey, c0):
    nc, S = K.nc, K.S
    with ExitStack() as ph:
        S.barrier()
        z = sbt(nc, ph, 'zz', [128, 512], F32)
        S.op('pool', lambda e: e.memset(z[:], 0.0), writes=['zz'])
        for t in range(K.ntile):
            S.dma('sp', Mix[t * 128:(t + 1) * 128, c0:c0 + 512], z[:], reads=['zz'], writes=[mkey])


def even_phase(K, i, U, ukey, Mix, mkey):
    if 'hgrn' in SKIP:
        zero_mix(K, Mix, mkey, 512)
    else:
        hgrn_pass(K, i, U, ukey, Mix, mkey)
    if 'rwkv' in SKIP:
        zero_mix(K, Mix, mkey, 0)
    else:
        rwkv_pass(K, i, U, ukey, Mix, mkey)


def tile_order(K, d):
    if d == 0:
        return list(range(K.ntile))
    return list(range(K.nct - 1, -1, -1)) + list(range(K.ntile - 1, K.nct - 1, -1))


def load_masks(K, ph, idxs):
    nc, S = K.nc, K.S
    mk = sbt(nc, ph, 'masks', [128, len(idxs), 128], F32)
    for n, ix in enumerate(idxs):
        S.dma('sp', mk[:, n, :], K.I['masks'][ix], writes=['masks'])
    return mk


def hgrn_pass(K, i, U, ukey, Mix, mkey):
    nc, S, I = K.nc, K.S, K.I
    HQ = 1888
    with ExitStack() as ph:
        S.barrier()
        mk = load_masks(K, ph, [4, 5, 6])
        ones = sbt(nc, ph, 'ones', [128, 1], F32)
        S.op('pool', lambda e: e.memset(ones[:], 1.0), writes=['ones'])
        epsr = sbt(nc, ph, 'epsr', [128, 1], F32)
        S.op('pool', lambda e: e.memset(epsr[:], 1e-5), writes=['epsr'])
        lbt = sbt(nc, ph, 'lbt', [128, 3, 512], F32)
        lbv = sbt(nc, ph, 'lbv', [128, 2, 512], F32)
        ngb = sbt(nc, ph, 'ngb', [128, 128], F32)
        S.dma('sp', lbt[:], I['hg_lb'].rearrange("a b -> (a b)").partition_broadcast(128), writes=['lbt'])
        S.dma('sp', ngb[:], I['hg_norm_g'].partition_broadcast(128), writes=['ngb'])
        S.op('act', lambda e: e.activation(lbt[:], lbt[:], AF.Exp), reads=['lbt'], writes=['lbt'])
        S.op('dve', lambda e: e.tensor_tensor(lbv[:, 1, :], lbt[:, 0, :], lbt[:, 1, :], ALU.add), reads=['lbt'], writes=['lbv'])
        S.op('dve', lambda e: e.tensor_tensor(lbv[:, 1, :], lbv[:, 1, :], lbt[:, 2, :], ALU.add), reads=['lbt', 'lbv'], writes=['lbv'])
        S.op('dve', lambda e: e.reciprocal(lbv[:, 1, :], lbv[:, 1, :]), reads=['lbv'], writes=['lbv'])
        S.op('dve', lambda e: e.tensor_tensor(lbv[:, 0, :], lbt[:, 0, :], lbv[:, 1, :], ALU.mult), reads=['lbt', 'lbv'], writes=['lbv'])
        S.op('dve', lambda e: e.tensor_scalar(lbv[:, 1, :], lbv[:, 0, :], -1.0, 1.0, ALU.mult, ALU.add), reads=['lbv'], writes=['lbv'])
        ut = sbt(nc, ph, 'ut', [128, 2, 2560], F32)
        fg = sbt(nc, ph, 'fg', [128, 512], F32)
        lf = sbt(nc, ph, 'lf', [128, 512], F32)
        kk = sbt(nc, ph, 'kk', [128, 512], F32)
        qs = sbt(nc, ph, 'qs', [128, 512], F32)
        qb = sbt(nc, ph, 'qb', [128, 512], F32)
        kb = sbt(nc, ph, 'kb', [128, 512], F32)
        kd = sbt(nc, ph, 'kd', [128, 512], F32)
        ex = sbt(nc, ph, 'ex', [128, 512], F32)
        qbT = sbt(nc, ph, 'qbT', [128, 4, 128], F32)
        kbT = sbt(nc, ph, 'kbT', [128, 4, 128], F32)
        qbTc = sbt(nc, ph, 'qbTc', [128, 2, 4, 128], F32)
        S.op('pool', lambda e: e.memset(qbTc[:], 0.0), writes=['qbTc'])
        bl = sbt(nc, ph, 'bl', [128, 512], F32)
        kdm = sbt(nc, ph, 'kdm', [128, 2, 512], F32)
        rmask = sbt(nc, ph, 'rmask', [128, 4], F32)
        S.dma('sp', rmask[:], I['rowmask'], writes=['rmask'])
        scm = sbt(nc, ph, 'scm', [128, 4, 128], F32)
        ebl = sbt(nc, ph, 'ebl', [128, 8], F32)
        St = [sbt(nc, ph, 'St0', [128, 4, 128], F32), sbt(nc, ph, 'St1', [128, 4, 128], F32)]
        K.St2 = sbt(nc, ph, 'St2', [128, 4, 128], F32)
        yo = sbt(nc, ph, 'yo', [128, 2, 512], F32)
        yf = sbt(nc, ph, 'yf', [128, 2, 512], F32)
        sq = sbt(nc, ph, 'sq', [128, 512], F32)
        s2 = sbt(nc, ph, 's2', [128, 4], F32)
        sgt = sbt(nc, ph, 'sgt', [128, 512], F32)
        cnt = 0
        for d in range(2):
            S.op('pool', lambda e: e.memset(St[0][:], 0.0), writes=['St0'])
            cur = 0
            for t in tile_order(K, d):
                b = cnt % 2
                cnt += 1
                if t == K.nct - 1 and d == 1 or False:
                    pass
                S.dma('sp', ut[:, b, :], U[t * 128:(t + 1) * 128, HQ:HQ + 2560], reads=[ukey], writes=['ut%d' % b])
                uk = 'ut%d' % b
                q_ap = ut[:, b, 0:512]
                f_ap = ut[:, b, 512 + d * 512:1024 + d * 512]
                iv = ut[:, b, 1536:2048]
                g_ap = ut[:, b, 2048:2560]
                mc = mk[:, d, :]
                S.op('act', lambda e: e.activation(fg[:], f_ap, AF.Sigmoid), reads=[uk], writes=['fg'])
                S.op('dve', lambda e: e.tensor_tensor(fg[:], fg[:], lbv[:, 1, :], ALU.mult), reads=['fg', 'lbv'], writes=['fg'])
                S.op('dve', lambda e: e.tensor_tensor(fg[:], fg[:], lbv[:, 0, :], ALU.add), reads=['fg', 'lbv'], writes=['fg'])
                S.op('act', lambda e: e.activation(lf[:], fg[:], AF.Ln), reads=['fg'], writes=['lf'])
                S.op('dve', lambda e: e.tensor_scalar(kk[:], fg[:], -1.0, 1.0, ALU.mult, ALU.add), reads=['fg'], writes=['kk'])
                S.op('act', lambda e: e.activation(qs[:], q_ap, AF.Silu), reads=[uk], writes=['qs'])
                S.op('pe', lambda e: e.matmul(K.PS[0][:, :], mc, lf[:], start=True, stop=True), reads=['masks', 'lf'], writes=['B0'])
                S.op('pe', lambda e: e.matmul(K.PS[1][:, :], mk[:, 2, :], lf[:], start=True, stop=True), reads=['masks', 'lf'], writes=['B1'])
                S.op('act', lambda e: e.activation(ex[:], K.PS[0][:, :], AF.Exp), reads=['B0'], writes=['ex'])
                S.op('dve', lambda e: e.tensor_tensor(qb[:], qs[:], ex[:], ALU.mult), reads=['qs', 'ex'], writes=['qb'])
                S.op('act', lambda e: e.activation(ex[:], K.PS[0][:, :], AF.Exp, scale=-1.0), reads=['B0', 'qb'], writes=['ex'])
                S.op('dve', lambda e: e.tensor_tensor(kb[:], kk[:], ex[:], ALU.mult), reads=['kk', 'ex'], writes=['kb'])
                S.op('dve', lambda e: e.tensor_copy(bl[:], K.PS[1][:, :]), reads=['B1'], writes=['bl'])
                S.op('dve', lambda e: e.tensor_tensor(kd[:], bl[:], K.PS[0][:, :], ALU.subtract), reads=['B0', 'bl'], writes=['kd'])
                S.op('act', lambda e: e.activation(kd[:], kd[:], AF.Exp), reads=['kd'], writes=['kd'])
                S.op('pool', lambda e: e.tensor_tensor(kd[:], kd[:], kk[:], ALU.mult), reads=['kd', 'kk'], writes=['kd'])
                for c in range(2):
                    S.op('dve', lambda e: e.tensor_scalar(kdm[:, c, :], kd[:], rmask[:, 2 + c:3 + c], None, ALU.mult), reads=['kd', 'rmask'], writes=['kdm'])
                for hh in range(4):
                    S.op('pe', lambda e: e.transpose(K.PS[2][:, hh * 128:(hh + 1) * 128], qb[:, hh * 128:(hh + 1) * 128], K.ident[:]),
                         reads=['qb', 'ident'], writes=['B2'])
                for hh in range(4):
                    S.op('pe', lambda e: e.transpose(K.PS[3][:, hh * 128:(hh + 1) * 128], kb[:, hh * 128:(hh + 1) * 128], K.ident[:]),
                         reads=['kb', 'ident'], writes=['B3'])
                S.op('act', lambda e: e.copy(qbT[:].rearrange("p a b -> p (a b)"), K.PS[2][:, :]), reads=['B2'], writes=['qbT'])
                S.op('dve', lambda e: e.tensor_copy(kbT[:].rearrange("p a b -> p (a b)"), K.PS[3][:, :]), reads=['B3'], writes=['kbT'])
                for c in range(2):
                    S.op('dve', lambda e: e.tensor_copy(qbTc[:, c, :, c * 64:(c + 1) * 64], qbT[:, :, c * 64:(c + 1) * 64]),
                         reads=['qbT'], writes=['qbTc'])
                for hh in range(4):
                    S.op('pe', lambda e: e.transpose(K.PS[7][:, hh * 128:(hh + 1) * 128], bl[:, hh * 128:(hh + 1) * 128], K.ident[:]),
                         reads=['bl', 'ident'], writes=['B7'])
                S.op('act', lambda e: e.activation(ebl[:].rearrange("p (a c) -> p a c", c=2),
                                                   K.PS[7][:, :].rearrange("p (a c b) -> p a c b", a=4, c=2)[:, :, :, 0], AF.Exp), reads=['B7'], writes=['ebl'])
                for hh in range(4):
                    S.op('pe', lambda e: e.matmul(K.PS[4][:, hh * 128:(hh + 1) * 128], kbT[:, hh, :], qbT[:, hh, :], start=True, stop=True),
                         reads=['kbT', 'qbT'], writes=['B4'])
                S.op('dve', lambda e: e.tensor_tensor(scm[:], K.PS[4][:, :].rearrange("p (a b) -> p a b", a=4),
                                                      mc.unsqueeze(1).to_broadcast([128, 4, 128]), ALU.mult),
                     reads=['B4', 'masks'], writes=['scm'])
                chunks = [0, 1] if d == 0 else [1, 0]
                sts = [(St[cur], 'St%d' % cur), (St[1 - cur], 'St%d' % (1 - cur))]
                for ci, c in enumerate(chunks):
                    sa, ka = sts[ci]
                    sb_, kb_ = (sts[1] if ci == 0 else (K.St2, 'St2'))
                    for hh in range(4):
                        S.op('pe', lambda e: e.matmul(K.PS[6][:, hh * 128:(hh + 1) * 128], kdm[:, c, hh * 128:(hh + 1) * 128],
                                                      iv[:, hh * 128:(hh + 1) * 128], start=True, stop=True),
                             reads=['kdm', uk], writes=['B6'])
                    eb_ap = ebl[:].rearrange("p (a c) -> p a c", c=2)[:, :, c:c + 1].to_broadcast([128, 4, 128])
                    S.op('dve', lambda e: e.tensor_tensor(sb_[:], sa[:], eb_ap, ALU.mult), reads=[ka, 'ebl'], writes=[kb_])
                    S.op('dve', lambda e: e.tensor_tensor(sb_[:], sb_[:], K.PS[6][:, :].rearrange("p (a b) -> p a b", a=4), ALU.add),
                         reads=[kb_, 'B6'], writes=[kb_])
                for hh in range(4):
                    for ci, c in enumerate(chunks):
                        sa, ka = sts[ci]
                        S.op('pe', lambda e: e.matmul(K.PS[5][:, hh * 128:(hh + 1) * 128], qbTc[:, c, hh, :], sa[:, hh, :],
                                                      start=(ci == 0), stop=False),
                             reads=['qbTc', ka], writes=['B5'])
                    S.op('pe', lambda e: e.matmul(K.PS[5][:, hh * 128:(hh + 1) * 128], scm[:, hh, :], iv[:, hh * 128:(hh + 1) * 128], start=False, stop=True),
                         reads=['scm', uk], writes=['B5'])
                S.op('act', lambda e: e.copy(St[cur][:], K.St2[:]), reads=['St2'], writes=['St%d' % cur])
                if d == 0:
                    S.op('act', lambda e: e.copy(yo[:, b, :], K.PS[5][:, :]), reads=['B5'], writes=['yo%d' % b])
                    S.dma('sp', K.Yf[t * 128:(t + 1) * 128, 512:1024], yo[:, b, :], reads=['yo%d' % b], writes=['Yf'])
                else:
                    S.dma('sp', yf[:, b, :], K.Yf[t * 128:(t + 1) * 128, 512:1024], reads=['Yf'], writes=['yf%d' % b])
                    S.op('dve', lambda e: e.tensor_tensor(yo[:, b, :], K.PS[5][:, :], yf[:, b, :], ALU.add), reads=['B5', 'yf%d' % b], writes=['yo%d' % b])
                    S.op('pool', lambda e: e.tensor_tensor(sq[:], yo[:, b, :], yo[:, b, :], ALU.mult), reads=['yo%d' % b], writes=['sq'])
                    S.op('dve', lambda e: e.reduce_sum(s2[:], sq[:].rearrange("p (a b) -> p a b", a=4), AX.X), reads=['sq'], writes=['s2'])
                    S.op('act', lambda e: e.activation(s2[:], s2[:], AF.Sqrt, bias=epsr[:], scale=1.0 / 128.0), reads=['s2', 'epsr'], writes=['s2'])
                    S.op('dve', lambda e: e.reciprocal(s2[:], s2[:]), reads=['s2'], writes=['s2'])
                    S.op('act', lambda e: e.activation(sgt[:], g_ap, AF.Silu), reads=[uk], writes=['sgt'])
                    yv = yo[:, b, :].rearrange("p (a b) -> p a b", a=4)
                    S.op('dve', lambda e: e.tensor_tensor(yv, yv, s2[:].unsqueeze(2).to_broadcast([128, 4, 128]), ALU.mult), reads=['yo%d' % b, 's2'], writes=['yo%d' % b])
                    S.op('pool', lambda e: e.tensor_tensor(yv, yv, ngb[:].unsqueeze(1).to_broadcast([128, 4, 128]), ALU.mult), reads=['yo%d' % b, 'ngb'], writes=['yo%d' % b])
                    S.op('dve', lambda e: e.tensor_tensor(yo[:, b, :], yo[:, b, :], sgt[:], ALU.mult), reads=['yo%d' % b, 'sgt'], writes=['yo%d' % b])
                    S.dma('sp', Mix[t * 128:(t + 1) * 128, 512:1024], yo[:, b, :], reads=['yo%d' % b], writes=[mkey])


def rwkv_pass(K, i, U, ukey, Mix, mkey):
    nc, S, I = K.nc, K.S, K.I
    A = 1888
    PS = K.PS
    C = -math.exp(-0.5)
    with ExitStack() as ph:
        S.barrier()
        mk = load_masks(K, ph, [0, 1, 2, 3])
        onesm = sbt(nc, ph, 'onesm', [128, 128], F32)
        S.op('pool', lambda e: e.memset(onesm[:], 1.0), writes=['onesm'])

        def bc(name, src, n):
            tl = sbt(nc, ph, name, [128, n], F32)
            S.dma('sp', tl[:], src.partition_broadcast(128), writes=[name])
            return tl
        mub = bc('mub', I['rw_mu'], A)
        w0b = bc('w0b', I['rw_w0'].rearrange("a b -> (a b)"), 1024)
        a0b = bc('a0b', I['rw_a0'], 512)
        kkb = bc('kkb', I['rw_k_k'], 512)
        kab = bc('kab', I['rw_k_a'], 512)
        rkb = bc('rkb', I['rw_r_k'], 512)
        gng = bc('gng', I['rw_gn_g'], 512)
        gnb = bc('gnb', I['rw_gn_b'], 512)
        w2s = sbt(nc, ph, 'w2s', [128, 2, 512], F32)
        a2s = sbt(nc, ph, 'a2s', [128, 512], F32)
        g2s = sbt(nc, ph, 'g2s', [128, 512], F32)
        g2b = sbt(nc, ph, 'g2b', [128, 512], F32)
        S.dma('sp', w2s[:, 0, :], I['rw_w2p'][0], writes=['w2s'])
        S.dma('sp', w2s[:, 1, :], I['rw_w2p'][1], writes=['w2s'])
        S.dma('sp', a2s[:], I['rw_a2p'], writes=['a2s'])
        S.dma('sp', g2s[:], I['rw_g2a'], writes=['g2s'])
        S.dma('sp', g2b[:], I['rw_g2b'], writes=['g2b'])
        rmask = sbt(nc, ph, 'rmask', [128, 4], F32)
        S.dma('sp', rmask[:], I['rowmask'], writes=['rmask'])
        zrow = sbt(nc, ph, 'zrow', [1, 1888], F32)
        S.op('pool', lambda e: e.memset(zrow[:], 0.0), writes=['zrow'])
        S.dma('sp', K.Upad[0:1, 0:1888], zrow[:], reads=['zrow', ukey], writes=[ukey])
        S.dma('sp', K.Upad[K.NT + 1:K.NT + 2, 0:1888], zrow[:], reads=['zrow', ukey], writes=[ukey])
        epsg = sbt(nc, ph, 'epsg', [128, 1], F32)
        S.op('pool', lambda e: e.memset(epsg[:], 64e-5), writes=['epsg'])

        def T(name, shape=(128, 512)):
            return sbt(nc, ph, name, list(shape), F32)
        uc, up, un, s1, xs = T('uc', (128, A)), T('up', (128, A)), T('un', (128, A)), T('s1', (128, A)), T('xs', (128, A))
        lt, ltT = T('lt', (128, 512)), T('ltT', (128, 4, 128))
        S.op('pool', lambda e: e.memset(lt[:], 0.0), writes=['lt'])
        ZtTm, RtTm, BtTm, KtTm = T('ZtTm', (128, 2, 4, 128)), T('RtTm', (128, 2, 4, 128)), T('BtTm', (128, 2, 4, 128)), T('KtTm', (128, 2, 4, 128))
        tmpS = T('tmpS', (128, 4, 64))
        lw, aa, gg, kk, kf, bq, tmp, tmp2 = T('lw'), T('aa'), T('gg'), T('kk'), T('kf'), T('bq'), T('tmp'), T('tmp2')
        n8, r8 = T('n8', (128, 8)), T('r8', (128, 8))
        Zt, Rt, Bt, Kt, Bh, Kh = T('Zt'), T('Rt'), T('Bt'), T('Kt'), T('Bh'), T('Kh')
        ZtT, RtT, BtT, KtT = T('ZtT', (128, 4, 128)), T('RtT', (128, 4, 128)), T('BtT', (128, 4, 128)), T('KtT', (128, 4, 128))
        Nn, NT_, Akz, Abr, Akr = T('Nn', (128, 8, 128)), T('NT_', (128, 8, 128)), T('Akz', (128, 8, 128)), T('Abr', (128, 8, 128)), T('Akr', (128, 8, 128))
        Mb = [T('Mb0', (128, 8, 128)), T('Mb1', (128, 8, 128))]
        MTb = [T('MTb0', (128, 8, 128)), T('MTb1', (128, 8, 128))]
        Qb = [T('Qb0', (128, 8, 128)), T('Qb1', (128, 8, 128))]
        Wsb, Usb, yy, yf = T('Wsb'), T('Usb'), T('yy'), T('yf')
        St = T('St', (128, 4, 64))
        epl = T('epl', (128, 4))
        st8 = T('st8', (128, 4, 8))
        for d in range(2):
            S.op('pool', lambda e: e.memset(St[:], 0.0), writes=['St'])
            incl, strict, strictT = mk[:, d, :], mk[:, 2 + d, :], mk[:, 3 - d, :]
            for t in tile_order(K, d):
                r0 = t * 128
                S.dma('sp', uc[:], U[r0:r0 + 128, 0:A], reads=[ukey], writes=['uc'])
                S.dma('sp', up[:], K.Upad[r0:r0 + 128, 0:A], reads=[ukey], writes=['up'])
                S.dma('sp', un[:], K.Upad[r0 + 2:r0 + 130, 0:A], reads=[ukey], writes=['un'])
                if t == 0 or t == K.nct:
                    S.op('dve', lambda e: e.tensor_scalar(up[:], up[:], rmask[:, 0:1], None, ALU.mult), reads=['up', 'rmask'], writes=['up'])
                if t == K.nct - 1 or t == K.ntile - 1:
                    S.op('dve', lambda e: e.tensor_scalar(un[:], un[:], rmask[:, 1:2], None, ALU.mult), reads=['un', 'rmask'], writes=['un'])
                S.op('pool', lambda e: e.tensor_tensor(s1[:], up[:], un[:], ALU.add), reads=['up', 'un'], writes=['s1'])
                S.op('dve', lambda e: e.scalar_tensor_tensor(s1[:], s1[:], 0.5, uc[:], ALU.mult, ALU.subtract), reads=['s1', 'uc'], writes=['s1'])
                S.op('pool', lambda e: e.tensor_tensor(s1[:], s1[:], mub[:], ALU.mult), reads=['s1', 'mub'], writes=['s1'])
                S.op('dve', lambda e: e.tensor_tensor(xs[:], s1[:], uc[:], ALU.add), reads=['s1', 'uc'], writes=['xs'])
                r_ap, k0, v_ap = xs[:, 0:512], xs[:, 512:1024], xs[:, 1024:1536]
                S.op('act', lambda e: e.activation(lt[:, 0:128], xs[:, 1536:1664], AF.Tanh), reads=['xs'], writes=['lt'])
                S.op('act', lambda e: e.copy(lt[:, 128:192], xs[:, 1664:1728]), reads=['xs'], writes=['lt'])
                S.op('act', lambda e: e.activation(lt[:, 256:384], xs[:, 1728:1856], AF.Sigmoid), reads=['xs'], writes=['lt'])
                S.op('act', lambda e: e.activation(lt[:, 384:416], xs[:, 1856:1888], AF.Sigmoid), reads=['xs'], writes=['lt'])
                for n in range(4):
                    S.op('pe', lambda e: e.transpose(PS[3][:, n * 128:(n + 1) * 128], lt[:, n * 128:(n + 1) * 128], K.ident[:]), reads=['lt', 'ident'], writes=['B3'])
                S.op('dve', lambda e: e.tensor_copy(ltT[:].rearrange("p a b -> p (a b)"), PS[3][:, :]), reads=['B3'], writes=['ltT'])
                S.op('pe', lambda e: e.matmul(PS[0][:, :], ltT[:, 0, :], w2s[:, d, :], start=True, stop=True), reads=['ltT', 'w2s'], writes=['B0'])
                S.op('pe', lambda e: e.matmul(PS[1][:, :], ltT[:, 1, :], a2s[:, :], start=True, stop=True), reads=['ltT', 'a2s'], writes=['B1'])
                S.op('pe', lambda e: e.matmul(PS[2][:, :], ltT[:, 2, :], g2s[:, :], start=True, stop=False), reads=['ltT', 'g2s'], writes=['B2'])
                S.op('pe', lambda e: e.matmul(PS[2][:, :], ltT[:, 3, :], g2b[:, :], start=False, stop=True), reads=['ltT', 'g2b'], writes=['B2'])
                S.op('dve', lambda e: e.tensor_tensor(lw[:], PS[0][:, :], w0b[:, d * 512:(d + 1) * 512], ALU.add), reads=['B0', 'w0b'], writes=['lw'])
                S.op('act', lambda e: e.activation(lw[:], lw[:], AF.Sigmoid), reads=['lw'], writes=['lw'])
                S.op('pool', lambda e: e.tensor_scalar_mul(lw[:], lw[:], float(C)), reads=['lw'], writes=['lw'])
                S.op('dve', lambda e: e.tensor_tensor(aa[:], PS[1][:, :], a0b[:], ALU.add), reads=['B1', 'a0b'], writes=['aa'])
                S.op('act', lambda e: e.activation(aa[:], aa[:], AF.Sigmoid), reads=['aa'], writes=['aa'])
                S.op('act', lambda e: e.copy(gg[:], PS[2][:, :]), reads=['B2'], writes=['gg'])
                S.op('dve', lambda e: e.tensor_tensor(kk[:], k0, kkb[:], ALU.mult), reads=['xs', 'kkb'], writes=['kk'])
                S.op('pool', lambda e: e.tensor_tensor(tmp[:], kk[:], kk[:], ALU.mult), reads=['kk'], writes=['tmp'])
                S.op('dve', lambda e: e.reduce_sum(n8[:], tmp[:].rearrange("p (a b) -> p a b", a=8), AX.X), reads=['tmp'], writes=['n8'])
                S.op('act', lambda e: e.activation(n8[:], n8[:], AF.Sqrt), reads=['n8'], writes=['n8'])
                S.op('dve', lambda e: e.tensor_scalar_max(n8[:], n8[:], 1e-12), reads=['n8'], writes=['n8'])
                S.op('dve', lambda e: e.reciprocal(n8[:], n8[:]), reads=['n8'], writes=['n8'])
                kkv = kk[:].rearrange("p (a b) -> p a b", a=8)
                S.op('dve', lambda e: e.tensor_tensor(kkv, kkv, n8[:].unsqueeze(2).to_broadcast([128, 8, 64]), ALU.mult), reads=['kk', 'n8'], writes=['kk'])
                S.op('dve', lambda e: e.scalar_tensor_tensor(kf[:], aa[:], -1.0, kab[:], ALU.add, ALU.mult), reads=['aa', 'kab'], writes=['kf'])
                S.op('dve', lambda e: e.scalar_tensor_tensor(kf[:], kf[:], 1.0, k0, ALU.add, ALU.mult), reads=['kf', 'xs'], writes=['kf'])
                S.op('pool', lambda e: e.tensor_tensor(bq[:], kk[:], aa[:], ALU.mult), reads=['kk', 'aa'], writes=['bq'])
                S.op('pe', lambda e: e.matmul(PS[0][:, :], incl, lw[:], start=True, stop=True), reads=['masks', 'lw'], writes=['B0'])
                S.op('pe', lambda e: e.matmul(PS[1][:, :], onesm[:], lw[:], start=True, stop=True), reads=['onesm', 'lw'], writes=['B1'])
                S.op('act', lambda e: e.activation(tmp[:], PS[0][:, :], AF.Exp), reads=['B0'], writes=['tmp'])
                S.op('dve', lambda e: e.tensor_tensor(Rt[:], r_ap, tmp[:], ALU.mult), reads=['xs', 'tmp'], writes=['Rt'])
                S.op('act', lambda e: e.activation(tmp2[:], PS[0][:, :], AF.Exp, scale=-1.0), reads=['B0'], writes=['tmp2'])
                S.op('dve', lambda e: e.tensor_tensor(Kt[:], kf[:], tmp2[:], ALU.mult), reads=['kf', 'tmp2'], writes=['Kt'])
                S.op('pool', lambda e: e.tensor_tensor(Bt[:], bq[:], tmp2[:], ALU.mult), reads=['bq', 'tmp2'], writes=['Bt'])
                S.op('dve', lambda e: e.tensor_tensor(tmp[:], PS[0][:, :], lw[:], ALU.subtract), reads=['B0', 'lw', 'Rt'], writes=['tmp'])
                S.op('act', lambda e: e.activation(tmp[:], tmp[:], AF.Exp), reads=['tmp'], writes=['tmp'])
                S.op('dve', lambda e: e.scalar_tensor_tensor(Zt[:], kk[:], -1.0, tmp[:], ALU.mult, ALU.mult), reads=['kk', 'tmp'], writes=['Zt'])
                S.op('act', lambda e: e.copy(tmp2[:], PS[1][:, :]), reads=['B1', 'Kt', 'Bt'], writes=['tmp2'])
                for p in range(4):
                    S.op('pe', lambda e: e.transpose(PS[4][:, p * 128:(p + 1) * 128], tmp2[:, p * 128:(p + 1) * 128], K.ident[:]), reads=['tmp2', 'ident'], writes=['B4'])
                S.op('act', lambda e: e.activation(epl[:], PS[4][:, :].rearrange("p (a b) -> p a b", a=4)[:, :, 0], AF.Exp), reads=['B4'], writes=['epl'])
                S.op('dve', lambda e: e.tensor_tensor(tmp2[:], tmp2[:], PS[0][:, :], ALU.subtract), reads=['tmp2', 'B0'], writes=['tmp2'])
                S.op('act', lambda e: e.activation(tmp2[:], tmp2[:], AF.Exp), reads=['tmp2'], writes=['tmp2'])
                S.op('dve', lambda e: e.tensor_tensor(Kh[:], kf[:], tmp2[:], ALU.mult), reads=['kf', 'tmp2'], writes=['Kh'])
                S.op('pool', lambda e: e.tensor_tensor(Bh[:], bq[:], tmp2[:], ALU.mult), reads=['bq', 'tmp2'], writes=['Bh'])
                for n, (src, skey, dst, dkey, dm, dmkey) in enumerate(((Zt, 'Zt', ZtT, 'ZtT', ZtTm, 'ZtTm'), (Rt, 'Rt', RtT, 'RtT', RtTm, 'RtTm'),
                                                                       (Bt, 'Bt', BtT, 'BtT', BtTm, 'BtTm'), (Kt, 'Kt', KtT, 'KtT', KtTm, 'KtTm'))):
                    pb = PS[2 + n]
                    for p in range(4):
                        S.op('pe', lambda e: e.transpose(pb[:, p * 128:(p + 1) * 128], src[:, p * 128:(p + 1) * 128], K.ident[:]), reads=[skey, 'ident'], writes=['B%d' % (2 + n)])
                    S.op('act', lambda e: e.copy(dst[:].rearrange("p a b -> p (a b)"), pb[:, :]), reads=['B%d' % (2 + n)], writes=[dkey])
                    for hp in range(2):
                        S.op('dve', lambda e: e.tensor_scalar(dm[:, hp].rearrange("p a b -> p (a b)"), pb[:, :], rmask[:, 2 + hp:3 + hp], None, ALU.mult),
                             reads=['B%d' % (2 + n), 'rmask'], writes=[dmkey])

                def pairmm(lhsm, lkey, rhs, rkey, dst, dkey, mask, banks):
                    for hf in range(2):
                        pb = PS[banks[hf]]
                        for q in range(4):
                            h = hf * 4 + q
                            S.op('pe', lambda e: e.matmul(pb[:, q * 128:(q + 1) * 128], lhsm[:, h % 2, h // 2, :], rhs[:, h // 2, :], start=True, stop=True),
                                 reads=[lkey, rkey], writes=['B%d' % banks[hf]])
                        S.op('dve', lambda e: e.tensor_tensor(dst[:, hf * 4:(hf + 1) * 4, :], pb[:, :].rearrange("p (a b) -> p a b", a=4),
                                                              mask.unsqueeze(1).to_broadcast([128, 4, 128]), ALU.mult),
                             reads=['B%d' % banks[hf], 'masks'], writes=[dkey])
                pairmm(BtTm, 'BtTm', ZtT, 'ZtT', Nn, 'Nn', strict, (6, 7))
                pairmm(ZtTm, 'ZtTm', BtT, 'BtT', NT_, 'NT_', strictT, (4, 5))
                pairmm(KtTm, 'KtTm', ZtT, 'ZtT', Akz, 'Akz', strict, (2, 3))
                pairmm(BtTm, 'BtTm', RtT, 'RtT', Abr, 'Abr', incl, (6, 7))
                pairmm(KtTm, 'KtTm', RtT, 'RtT', Akr, 'Akr', incl, (4, 5))
                S.op('pool', lambda e: e.tensor_tensor(Qb[0][:], Nn[:], K.ident[:].unsqueeze(1).to_broadcast([128, 8, 128]), ALU.add), reads=['Nn', 'ident'], writes=['Qb0'])
                M, MT, mkey_, mtkey = Nn, NT_, 'Nn', 'NT_'
                qi = 0
                for j in range(6):
                    M2, MT2 = Mb[j % 2], MTb[j % 2]
                    k2, kt2 = 'Mb%d' % (j % 2), 'MTb%d' % (j % 2)
                    for hf in range(2):
                        for q in range(4):
                            h = hf * 4 + q
                            S.op('pe', lambda e: e.matmul(PS[2 + hf][:, q * 128:(q + 1) * 128], MT[:, h, :], M[:, h, :], start=True, stop=True),
                                 reads=[mkey_, mtkey], writes=['B%d' % (2 + hf)])
                        S.op('act', lambda e: e.copy(M2[:, hf * 4:(hf + 1) * 4, :], PS[2 + hf][:, :].rearrange("p (a b) -> p a b", a=4)), reads=['B%d' % (2 + hf)], writes=[k2])
                    for hf in range(2):
                        for q in range(4):
                            h = hf * 4 + q
                            S.op('pe', lambda e: e.matmul(PS[4 + hf][:, q * 128:(q + 1) * 128], M[:, h, :], MT[:, h, :], start=True, stop=True),
                                 reads=[mkey_, mtkey], writes=['B%d' % (4 + hf)])
                        S.op('dve', lambda e: e.tensor_copy(MT2[:, hf * 4:(hf + 1) * 4, :], PS[4 + hf][:, :].rearrange("p (a b) -> p a b", a=4)), reads=['B%d' % (4 + hf)], writes=[kt2])
                    Qo, Qn = Qb[qi], Qb[1 - qi]
                    for hf in range(2):
                        for q in range(4):
                            h = hf * 4 + q
                            S.op('pe', lambda e: e.matmul(PS[6 + hf][:, q * 128:(q + 1) * 128], MT2[:, h, :], Qo[:, h, :], start=True, stop=True),
                                 reads=[kt2, 'Qb%d' % qi], writes=['B%d' % (6 + hf)])
                        S.op('dve', lambda e: e.tensor_tensor(Qn[:, hf * 4:(hf + 1) * 4, :], Qo[:, hf * 4:(hf + 1) * 4, :],
                                                              PS[6 + hf][:, :].rearrange("p (a b) -> p a b", a=4), ALU.add),
                             reads=['B%d' % (6 + hf), 'Qb%d' % qi], writes=['Qb%d' % (1 - qi)])
                    qi = 1 - qi
                    M, MT, mkey_, mtkey = M2, MT2, k2, kt2
                Q, qkey = Qb[qi], 'Qb%d' % qi
                def sth(h):
                    return St[(h % 2) * 64:(h % 2 + 1) * 64, h // 2, :]
                for h in range(8):
                    S.op('pe', lambda e: e.matmul(PS[0][:, h * 64:(h + 1) * 64], ZtTm[:, h % 2, h // 2, :], St[:, h // 2, :], start=True, stop=False), reads=['ZtTm', 'St'], writes=['B0'])
                    S.op('pe', lambda e: e.matmul(PS[0][:, h * 64:(h + 1) * 64], Akz[:, h, :], v_ap[:, h * 64:(h + 1) * 64], start=False, stop=True), reads=['Akz', 'xs'], writes=['B0'])
                S.op('act', lambda e: e.copy(Wsb[:], PS[0][:, :]), reads=['B0'], writes=['Wsb'])
                for h in range(8):
                    S.op('pe', lambda e: e.matmul(PS[1][:, h * 64:(h + 1) * 64], Q[:, h, :], Wsb[:, h * 64:(h + 1) * 64], start=True, stop=True), reads=[qkey, 'Wsb'], writes=['B1'])
                S.op('dve', lambda e: e.tensor_copy(Usb[:], PS[1][:, :]), reads=['B1'], writes=['Usb'])
                for h in range(8):
                    S.op('pe', lambda e: e.matmul(PS[2][:, h * 64:(h + 1) * 64], RtTm[:, h % 2, h // 2, :], St[:, h // 2, :], start=True, stop=False), reads=['RtTm', 'St'], writes=['B2'])
                    S.op('pe', lambda e: e.matmul(PS[2][:, h * 64:(h + 1) * 64], Abr[:, h, :], Usb[:, h * 64:(h + 1) * 64], start=False, stop=False), reads=['Abr', 'Usb'], writes=['B2'])
                    S.op('pe', lambda e: e.matmul(PS[2][:, h * 64:(h + 1) * 64], Akr[:, h, :], v_ap[:, h * 64:(h + 1) * 64], start=False, stop=True), reads=['Akr', 'xs'], writes=['B2'])
                for p in range(4):
                    S.op('pe', lambda e: e.matmul(PS[3][:, p * 128:(p + 1) * 128], Bh[:, p * 128:(p + 1) * 128], Usb[:, p * 128:(p + 1) * 128], start=True, stop=False), reads=['Bh', 'Usb'], writes=['B3'])
                    S.op('pe', lambda e: e.matmul(PS[3][:, p * 128:(p + 1) * 128], Kh[:, p * 128:(p + 1) * 128], v_ap[:, p * 128:(p + 1) * 128], start=False, stop=True), reads=['Kh', 'xs'], writes=['B3'])
                ps3v = PS[3][:, :].rearrange("p (a b) -> p a b", a=4)
                S.op('dve', lambda e: e.tensor_scalar(tmpS[:], ps3v[:, :, 0:64], rmask[:, 2:3], None, ALU.mult), reads=['B3', 'rmask'], writes=['tmpS'])
                S.op('dve', lambda e: e.scalar_tensor_tensor(tmpS[:], ps3v[:, :, 64:128], rmask[:, 3:4], tmpS[:], ALU.mult, ALU.add), reads=['B3', 'rmask', 'tmpS'], writes=['tmpS'])
                S.op('dve', lambda e: e.tensor_tensor(St[:], St[:], epl[:].unsqueeze(2).to_broadcast([128, 4, 64]), ALU.mult), reads=['St', 'epl', 'B0', 'B2'], writes=['St'])
                S.op('dve', lambda e: e.tensor_tensor(St[:], St[:], tmpS[:], ALU.add), reads=['St', 'tmpS'], writes=['St'])
                if d == 0:
                    S.op('act', lambda e: e.copy(yy[:], PS[2][:, :]), reads=['B2'], writes=['yy'])
                    S.dma('sp', K.Yf[r0:r0 + 128, 0:512], yy[:], reads=['yy'], writes=['Yf'])
                else:
                    S.dma('sp', yf[:], K.Yf[r0:r0 + 128, 0:512], reads=['Yf'], writes=['yf'])
                    S.op('dve', lambda e: e.tensor_tensor(yy[:], PS[2][:, :], yf[:], ALU.add), reads=['B2', 'yf'], writes=['yy'])
                    yv = yy[:].rearrange("p (a b) -> p a b", a=8)
                    S.op('dve', lambda e: e.reduce_sum(st8[:, 0, :], yv, AX.X), reads=['yy'], writes=['st8'])
                    S.op('pool', lambda e: e.tensor_tensor(tmp[:], yy[:], yy[:], ALU.mult), reads=['yy'], writes=['tmp'])
                    S.op('dve', lambda e: e.reduce_sum(st8[:, 1, :], tmp[:].rearrange("p (a b) -> p a b", a=8), AX.X), reads=['tmp', 'st8'], writes=['st8'])
                    S.op('dve', lambda e: e.tensor_scalar_mul(st8[:, 0:2, :], st8[:, 0:2, :], 1.0 / 64.0), reads=['st8'], writes=['st8'])
                    S.op('dve', lambda e: e.tensor_tensor(st8[:, 2, :], st8[:, 0, :], st8[:, 0, :], ALU.mult), reads=['st8'], writes=['st8'])
                    S.op('dve', lambda e: e.tensor_tensor(st8[:, 2, :], st8[:, 1, :], st8[:, 2, :], ALU.subtract), reads=['st8'], writes=['st8'])
                    S.op('act', lambda e: e.activation(st8[:, 2, :], st8[:, 2, :], AF.Sqrt, bias=epsg[:], scale=1.0), reads=['st8', 'epsg'], writes=['st8'])
                    S.op('dve', lambda e: e.reciprocal(st8[:, 2, :], st8[:, 2, :]), reads=['st8'], writes=['st8'])
                    S.op('dve', lambda e: e.tensor_tensor(yv, yv, st8[:, 0, :].unsqueeze(2).to_broadcast([128, 8, 64]), ALU.subtract), reads=['yy', 'st8'], writes=['yy'])
                    S.op('dve', lambda e: e.tensor_tensor(yv, yv, st8[:, 2, :].unsqueeze(2).to_broadcast([128, 8, 64]), ALU.mult), reads=['yy', 'st8'], writes=['yy'])
                    S.op('pool', lambda e: e.tensor_tensor(yy[:], yy[:], gng[:], ALU.mult), reads=['yy', 'gng'], writes=['yy'])
                    S.op('pool', lambda e: e.tensor_tensor(yy[:], yy[:], gnb[:], ALU.add), reads=['yy', 'gnb'], writes=['yy'])
                    S.op('dve', lambda e: e.tensor_tensor(tmp[:], r_ap, kf[:], ALU.mult), reads=['xs', 'kf', 'st8'], writes=['tmp'])
                    S.op('pool', lambda e: e.tensor_tensor(tmp[:], tmp[:], rkb[:], ALU.mult), reads=['tmp', 'rkb'], writes=['tmp'])
                    S.op('dve', lambda e: e.reduce_sum(r8[:], tmp[:].rearrange("p (a b) -> p a b", a=8), AX.X), reads=['tmp'], writes=['r8'])
                    S.op('dve', lambda e: e.tensor_tensor(tmp2[:].rearrange("p (a b) -> p a b", a=8), v_ap.rearrange("p (a b) -> p a b", a=8),
                                                          r8[:].unsqueeze(2).to_broadcast([128, 8, 64]), ALU.mult), reads=['xs', 'r8', 'Kh', 'Bh'], writes=['tmp2'])
                    S.op('dve', lambda e: e.tensor_tensor(yy[:], yy[:], tmp2[:], ALU.add), reads=['yy', 'tmp2'], writes=['yy'])
                    S.op('dve', lambda e: e.tensor_tensor(yy[:], yy[:], gg[:], ALU.mult), reads=['yy', 'gg'], writes=['yy'])
                    S.dma('sp', Mix[r0:r0 + 128, 0:512], yy[:], reads=['yy'], writes=[mkey])


def phase_mod(K):
    nc, S, I = K.nc, K.S, K.I
    with ExitStack() as ph:
        S.barrier()
        cc = sbt(nc, ph, 'cc', [128, 8, 2], F32)
        sc = sbt(nc, ph, 'scc', [128, 8, 2], F32)
        bm = sbt(nc, ph, 'bm', [2, 9 * D], F32)
        mrow = sbt(nc, ph, 'mrow', [2, 9 * D], F32)
        wm = sbt(nc, ph, 'wm', [128, 2, 8, 512], F32)
        S.dma('sp', cc[:], I['cc'], writes=['cc'])
        S.op('act', lambda e: e.activation(sc[:], cc[:], AF.Silu), reads=['cc'], writes=['scc'])
        for i in range(2):
            S.dma('sp', bm[:], I['b_mod'][i, :].partition_broadcast(2), writes=['bm'])
            wv = I['w_mod'][i].rearrange("(k p) n -> p k n", p=128)
            for n in range(18):
                b = n % 2
                S.dma('sp' if b == 0 else 'act', wm[:, b], wv[:, :, n * 512:(n + 1) * 512], writes=['wm%d' % b])
                pm = K.PS[b]
                for k in range(8):
                    S.op('pe', lambda e: e.matmul(pm[0:2, :], sc[:, k, :], wm[:, b, k, :], start=(k == 0), stop=(k == 7)),
                         reads=['scc', 'wm%d' % b], writes=['B%d' % b])
                S.op('dve', lambda e: e.tensor_tensor(mrow[:, n * 512:(n + 1) * 512], pm[0:2, :], bm[:, n * 512:(n + 1) * 512], ALU.add),
                     reads=['B%d' % b, 'bm'], writes=['mrow'])
            S.dma('sp', K.Mrow[i], mrow[:], reads=['mrow'], writes=['Mrow'])


def load_mod_cols(K, ph, i, j, tagp):
    nc, S = K.nc, K.S
    cols = sbt(nc, ph, tagp + 'cols', [128, 2, 2, 8], F32)
    for kind in range(2):
        for q in range(2):
            S.dma('sp', cols[:, kind, q, :], K.Mrow[i, kind, (j * 3 + q) * D:(j * 3 + q + 1) * D].rearrange("(p k) -> p k", p=128),
                  reads=['Mrow'], writes=[tagp + 'cols'])
    S.op('dve', lambda e: e.tensor_scalar_add(cols[:, :, 1, :], cols[:, :, 1, :], 1.0), reads=[tagp + 'cols'], writes=[tagp + 'cols'])
    return cols


def load_post_rows(K, ph, i, j, weight, tagp):
    nc, S, I = K.nc, K.S, K.I
    gate = sbt(nc, ph, tagp + 'gate', [128, 2, D], F32)
    lng = sbt(nc, ph, tagp + 'lng', [128, D], F32)
    lnb = sbt(nc, ph, tagp + 'lnb', [128, D], F32)
    for kind in range(2):
        S.dma('sp', gate[:, kind, :], K.Mrow[i, kind, (j * 3 + 2) * D:(j * 3 + 3) * D].partition_broadcast(128),
              reads=['Mrow'], writes=[tagp + 'gate'])
    if weight != 1.0:
        S.op('pool', lambda e: e.tensor_scalar_mul(gate[:], gate[:], float(weight)), reads=[tagp + 'gate'], writes=[tagp + 'gate'])
    S.dma('sp', lng[:], I['ln_g'][i, j, :].partition_broadcast(128), writes=[tagp + 'lng'])
    S.dma('sp', lnb[:], I['ln_b'][i, j, :].partition_broadcast(128), writes=[tagp + 'lnb'])
    return gate, lng, lnb


def transpose_mod(K, hin_ap, hkey, cols, kind, aT, akey, col0):
    S = K.S
    for half in range(2):
        pb = K.PS[half]
        for q in range(4):
            kc = half * 4 + q
            S.op('pe', lambda e: e.transpose(pb[:, q * 128:(q + 1) * 128], hin_ap.rearrange("t (p k) -> t k p", k=8)[:, kc, :], K.ident[:]),
                 reads=[hkey, 'ident'], writes=['B%d' % half])
        for q in range(4):
            kc = half * 4 + q
            if q % 2 == 0:
                S.op('dve', lambda e: e.tensor_scalar(aT[:, kc, col0:col0 + 128], pb[:, q * 128:(q + 1) * 128],
                                                      cols[:, kind, 1, kc:kc + 1], cols[:, kind, 0, kc:kc + 1], ALU.mult, ALU.add),
                     reads=['B%d' % half, 'cols'], writes=[akey])
            else:
                S.op('act', lambda e: e.activation(aT[:, kc, col0:col0 + 128], pb[:, q * 128:(q + 1) * 128], AF.Identity,
                                                   bias=cols[:, kind, 0, kc:kc + 1], scale=cols[:, kind, 1, kc:kc + 1]),
                     reads=['B%d' % half, 'cols'], writes=[akey])


def postnorm_tile(K, W, po_banks, hin_ap, hkey, gate_ap, lng, lnb, hout_ap, hokey, wk):
    S = K.S
    t1, y, st6, mv, rstd = W['t1'], W['y'], W['st6'], W['mv'], W['rstd']
    for half in range(2):
        S.op('dve', lambda e: e.tensor_tensor(t1[:, half * 512:(half + 1) * 512], po_banks[half][:, :], gate_ap[:, half * 512:(half + 1) * 512], ALU.mult),
             reads=['B%d' % (6 + half), 'gate'], writes=[wk + 't1'])
    S.op('dve', lambda e: e.scalar_tensor_tensor(y[:], hin_ap, float(ALPHA), t1[:], ALU.mult, ALU.add),
         reads=[hkey, wk + 't1'], writes=[wk + 'y'])
    for half in range(2):
        S.op('dve', lambda e: e.bn_stats(st6[:, half, :], y[:, half * 512:(half + 1) * 512]), reads=[wk + 'y'], writes=[wk + 'st6'])
    S.op('dve', lambda e: e.bn_aggr(mv[:], st6[:]), reads=[wk + 'st6'], writes=[wk + 'mv'])
    S.op('act', lambda e: e.activation(rstd[:], mv[:, 1:2], AF.Sqrt, bias=K.epsc[:], scale=1.0), reads=[wk + 'mv', 'epsc'], writes=[wk + 'rstd'])
    S.op('dve', lambda e: e.reciprocal(rstd[:], rstd[:]), reads=[wk + 'rstd'], writes=[wk + 'rstd'])
    S.op('dve', lambda e: e.tensor_scalar(y[:], y[:], mv[:, 0:1], rstd[:], ALU.subtract, ALU.mult),
         reads=[wk + 'y', wk + 'mv', wk + 'rstd'], writes=[wk + 'y'])
    S.op('pool', lambda e: e.tensor_tensor(y[:], y[:], lng[:], ALU.mult), reads=[wk + 'y', 'lng'], writes=[wk + 'y'])
    S.op('pool', lambda e: e.tensor_tensor(hout_ap, y[:], lnb[:], ALU.add), reads=[wk + 'y', 'lnb'], writes=[hokey])


def ffn_phase(K, i, s, j, Hsrc, skey, Hdst, dkey, tiles):
    nc, S, I = K.nc, K.S, K.I
    G = 2
    with ExitStack() as ph:
        S.barrier()
        Win = sbt(nc, ph, 'Win', [128, 8, 2 * FF], BF16)
        Wout = sbt(nc, ph, 'Wout', [128, 22, D], BF16)
        wv = I['ffn_w_in'][i, s].rearrange("(p k) f -> p k f", p=128)
        for c in range(2):
            S.dma('pool', Win[:, :, c * FF:(c + 1) * FF], wv[:, :, c * FF:(c + 1) * FF], writes=['Win'])
        wo = I['ffn_w_out'][i, s].rearrange("(k p) d -> p k d", p=128)
        for c in range(2):
            S.dma('pool', Wout[:, c * 11:(c + 1) * 11, :], wo[:, c * 11:(c + 1) * 11, :], writes=['Wout'])
        cols = load_mod_cols(K, ph, i, j, '')
        gate, lng, lnb = load_post_rows(K, ph, i, j, 0.5, '')
        K.epsc = sbt(nc, ph, 'epsc', [128, 1], F32)
        S.op('pool', lambda e: e.memset(K.epsc[:], LN_EPS), writes=['epsc'])
        hin = sbt(nc, ph, 'hin', [128, 2, G, D], F32)
        aT = sbt(nc, ph, 'aT', [128, 8, G * 128], BF16)
        gT = sbt(nc, ph, 'gT', [128, 22, G * 128], BF16)
        sg = sbt(nc, ph, 'sg', [128, 2, G * 128], F32)
        hout = sbt(nc, ph, 'hout', [128, 2, D], F32)
        W = {'t1': sbt(nc, ph, 't1', [128, D], F32),
             'y': sbt(nc, ph, 'y', [128, D], F32),
             'st6': sbt(nc, ph, 'st6', [128, 2, 6], F32),
             'mv': sbt(nc, ph, 'mv', [128, 2], F32),
             'rstd': sbt(nc, ph, 'rstd', [128, 1], F32)}
        groups = []
        cur = []
        for t in tiles:
            kind = 1 if t < K.nct else 0
            if cur and (len(cur) == G or cur[0][1] != kind):
                groups.append(cur)
                cur = []
            cur.append((t, kind))
        if cur:
            groups.append(cur)
        nout = 0
        for gi, grp in enumerate(groups):
            hb = gi % 2
            kind = grp[0][1]
            ng = len(grp)
            ncol = ng * 128
            for ti, (t, _) in enumerate(grp):
                S.dma('sp', hin[:, hb, ti, :], Hsrc[t * 128:(t + 1) * 128, :], reads=[skey], writes=['hin%d' % hb])
            for ti in range(ng):
                transpose_mod(K, hin[:, hb, ti, :], 'hin%d' % hb, cols, kind, aT, 'aT', ti * 128)
            for fc in range(22):
                pg, pu = K.PS[2 + 2 * (fc % 2)], K.PS[3 + 2 * (fc % 2)]
                kg, ku = 'B%d' % (2 + 2 * (fc % 2)), 'B%d' % (3 + 2 * (fc % 2))
                for kc in range(8):
                    S.op('pe', lambda e: e.matmul(pg[:, :ncol], Win[:, kc, fc * 128:(fc + 1) * 128], aT[:, kc, :ncol], start=(kc == 0), stop=(kc == 7)),
                         reads=['Win', 'aT'], writes=[kg])
                for kc in range(8):
                    S.op('pe', lambda e: e.matmul(pu[:, :ncol], Win[:, kc, FF + fc * 128:FF + (fc + 1) * 128], aT[:, kc, :ncol], start=(kc == 0), stop=(kc == 7)),
                         reads=['Win', 'aT'], writes=[ku])
                sb = fc % 2
                S.op('act', lambda e: e.activation(sg[:, sb, :ncol], pg[:, :ncol], AF.Silu), reads=[kg], writes=['sg%d' % sb])
                S.op('dve', lambda e: e.tensor_tensor(gT[:, fc, :ncol], sg[:, sb, :ncol], pu[:, :ncol], ALU.mult),
                     reads=['sg%d' % sb, ku], writes=['gT'])
            for ti, (t, _) in enumerate(grp):
                for half in range(2):
                    po = K.PS[6 + half]
                    for fc in range(22):
                        S.op('pe', lambda e: e.matmul(po[:, :], gT[:, fc, ti * 128:(ti + 1) * 128], Wout[:, fc, half * 512:(half + 1) * 512], start=(fc == 0), stop=(fc == 21)),
                             reads=['gT', 'Wout'], writes=['B%d' % (6 + half)])
                ob = nout % 2
                nout += 1
                postnorm_tile(K, W, [K.PS[6], K.PS[7]], hin[:, hb, ti, :], 'hin%d' % hb, gate[:, kind, :], lng, lnb,
                              hout[:, ob, :], 'hout%d' % ob, '')
                S.dma('sp', Hdst[t * 128:(t + 1) * 128, :], hout[:, ob, :], reads=['hout%d' % ob], writes=[dkey])


def tile_kind(K, t):
    return 1 if t < K.nct else 0


def proj_phase(K, i, w_dram, N, Hsrc, skey, Udst, ukey, tiles):
    nc, S = K.nc, K.S
    with ExitStack() as ph:
        S.barrier()
        Wp = sbt(nc, ph, 'Wp', [128, 8, N], BF16)
        wv = w_dram.rearrange("(p k) f -> p k f", p=128)
        nch = (N + 511) // 512
        hN = N // 2
        for c in range(2):
            S.dma('pool', Wp[:, :, c * hN:(c + 1) * hN], wv[:, :, c * hN:(c + 1) * hN], writes=['Wp'])
        cols = load_mod_cols(K, ph, i, 1, '')
        hin = sbt(nc, ph, 'hin', [128, 2, D], F32)
        aT = sbt(nc, ph, 'aT', [128, 2, 8, 128], BF16)
        ub = sbt(nc, ph, 'ub', [128, 2, N], F32)
        for ti, t in enumerate(tiles):
            b = ti % 2
            kind = tile_kind(K, t)
            S.dma('sp', hin[:, b, :], Hsrc[t * 128:(t + 1) * 128, :], reads=[skey], writes=['hin%d' % b])
            transpose_mod(K, hin[:, b, :], 'hin%d' % b, cols, kind, aT[:, b], 'aT%d' % b, 0)
            for c in range(nch):
                c0, c1 = c * 512, min(N, (c + 1) * 512)
                pb = K.PS[2 + (c % 4)]
                for kc in range(8):
                    S.op('pe', lambda e: e.matmul(pb[:, :c1 - c0], aT[:, b, kc, :], Wp[:, kc, c0:c1], start=(kc == 0), stop=(kc == 7)),
                         reads=['aT%d' % b, 'Wp'], writes=['B%d' % (2 + (c % 4))])
                if c % 2 == 0:
                    S.op('act', lambda e: e.copy(ub[:, b, c0:c1], pb[:, :c1 - c0]), reads=['B%d' % (2 + (c % 4))], writes=['ub%d' % b])
                else:
                    S.op('dve', lambda e: e.tensor_copy(ub[:, b, c0:c1], pb[:, :c1 - c0]), reads=['B%d' % (2 + (c % 4))], writes=['ub%d' % b])
            S.dma('sp', Udst[t * 128:(t + 1) * 128, :], ub[:, b, :], reads=['ub%d' % b], writes=[ukey])


def outproj_phase(K, i, w_dram, Msrc, mkey, Hsrc, skey, Hdst, dkey, tiles):
    nc, S = K.nc, K.S
    with ExitStack() as ph:
        S.barrier()
        Wo = sbt(nc, ph, 'Wo', [128, 8, D], BF16)
        S.dma('pool', Wo[:], w_dram.rearrange("(p k) f -> p k f", p=128), writes=['Wo'])
        gate, lng, lnb = load_post_rows(K, ph, i, 1, 1.0, '')
        K.epsc = sbt(nc, ph, 'epsc', [128, 1], F32)
        S.op('pool', lambda e: e.memset(K.epsc[:], LN_EPS), writes=['epsc'])
        hin = sbt(nc, ph, 'hin', [128, 2, D], F32)
        mx = sbt(nc, ph, 'mx', [128, 2, D], F32)
        mT = sbt(nc, ph, 'mT', [128, 2, 8, 128], BF16)
        hout = sbt(nc, ph, 'hout', [128, 2, D], F32)
        W = {'t1': sbt(nc, ph, 't1', [128, D], F32), 'y': sbt(nc, ph, 'y', [128, D], F32),
             'st6': sbt(nc, ph, 'st6', [128, 2, 6], F32), 'mv': sbt(nc, ph, 'mv', [128, 2], F32),
             'rstd': sbt(nc, ph, 'rstd', [128, 1], F32)}
        for ti, t in enumerate(tiles):
            b = ti % 2
            kind = tile_kind(K, t)
            S.dma('sp', hin[:, b, :], Hsrc[t * 128:(t + 1) * 128, :], reads=[skey], writes=['hin%d' % b])
            S.dma('sp', mx[:, b, :], Msrc[t * 128:(t + 1) * 128, :], reads=[mkey], writes=['mx%d' % b])
            for half in range(2):
                pb = K.PS[half]
                for q in range(4):
                    kc = half * 4 + q
                    S.op('pe', lambda e: e.transpose(pb[:, q * 128:(q + 1) * 128], mx[:, b, :].rearrange("t (p k) -> t k p", k=8)[:, kc, :], K.ident[:]),
                         reads=['mx%d' % b, 'ident'], writes=['B%d' % half])
                if half == 0:
                    S.op('act', lambda e: e.copy(mT[:, b, 0:4, :], pb[:, :].rearrange("p (q c) -> p q c", q=4)), reads=['B0'], writes=['mT%d' % b])
                else:
                    S.op('dve', lambda e: e.tensor_copy(mT[:, b, 4:8, :], pb[:, :].rearrange("p (q c) -> p q c", q=4)), reads=['B1'], writes=['mT%d' % b])
            for half in range(2):
                po = K.PS[6 + half]
                for kc in range(8):
                    S.op('pe', lambda e: e.matmul(po[:, :], mT[:, b, kc, :], Wo[:, kc, half * 512:(half + 1) * 512], start=(kc == 0), stop=(kc == 7)),
                         reads=['mT%d' % b, 'Wo'], writes=['B%d' % (6 + half)])
            postnorm_tile(K, W, [K.PS[6], K.PS[7]], hin[:, b, :], 'hin%d' % b, gate[:, kind, :], lng, lnb, hout[:, b, :], 'hout%d' % b, '')
            S.dma('sp', Hdst[t * 128:(t + 1) * 128, :], hout[:, b, :], reads=['hout%d' % b], writes=[dkey])


def attn_phase(K, i, QKV, qkey, Mix, mkey):
    nc, S, I = K.nc, K.S, K.I
    NT, TL, TC, nct, nlt, ntile = K.NT, K.TL, K.TC, K.nct, K.nlt, K.ntile
    lam_init = 0.8 - 0.6 * math.exp(-0.3 * i)
    QT, KT = K.QT, K.KT
    with ExitStack() as ph:
        S.barrier()
        qk = sbt(nc, ph, 'qk', [128, 2, 2048], F32)
        cs = sbt(nc, ph, 'cs', [128, 2, 2, 64], F32)
        r1 = sbt(nc, ph, 'r1', [128, 2048], F32)
        r2 = sbt(nc, ph, 'r2', [128, 2048], F32)
        tb = sbt(nc, ph, 'tb', [128, 2, 2, 8, 128], BF16)
        for t in range(ntile):
            b = t % 2
            S.dma('sp', qk[:, b, :], QKV[t * 128:(t + 1) * 128, 0:2048], reads=[qkey], writes=['qk%d' % b])
            src = qk[:, b, :]
            skey = 'qk%d' % b
            if t >= nct:
                lt = t - nct
                S.dma('sp', cs[:, b, 0, :], I['rope_cos'][lt * 128:(lt + 1) * 128, :], writes=['cs%d' % b])
                S.dma('sp', cs[:, b, 1, :], I['rope_sin'][lt * 128:(lt + 1) * 128, :], writes=['cs%d' % b])
                xv = qk[:, b, :].rearrange("p (g h q e) -> p g h q e", g=32, h=2, q=2)
                r2v = r2[:].rearrange("p (g h q e) -> p g h q e", g=32, h=2, q=2)
                cosb = cs[:, b, 0, :].unsqueeze(1).to_broadcast([128, 32, 64])
                sv = cs[:, b, 1, :].rearrange("p (h q e) -> p h q e", h=2, q=2)
                S.op('dve', lambda e: e.tensor_tensor(r1[:].rearrange("p (g d) -> p g d", g=32), qk[:, b, :].rearrange("p (g d) -> p g d", g=32), cosb, ALU.mult),
                     reads=[skey, 'cs%d' % b], writes=['r1'])
                for q in range(2):
                    S.op('pool', lambda e: e.tensor_tensor(r2v[:, :, :, q, :], xv[:, :, :, 1 - q, :],
                                                           sv[:, :, q, :].unsqueeze(1).to_broadcast([128, 32, 2, 16]), ALU.mult),
                         reads=[skey, 'cs%d' % b], writes=['r2'])
                S.op('dve', lambda e: e.tensor_tensor(r1[:], r1[:], r2[:], ALU.add), reads=['r1', 'r2'], writes=['r1'])
                src = r1[:]
                skey = 'r1'
            for w in range(2):
                if w == 0 and t < nct:
                    continue
                for hh in range(2):
                    pb = K.PS[w * 2 + hh]
                    for q in range(4):
                        h = hh * 4 + q
                        S.op('pe', lambda e: e.transpose(pb[:, q * 128:(q + 1) * 128], src[:, w * 1024 + h * 128: w * 1024 + (h + 1) * 128], K.ident[:]),
                             reads=[skey, 'ident'], writes=['B%d' % (w * 2 + hh)])
                    eng = 'act' if hh == 0 else 'dve'
                    if eng == 'act':
                        S.op('act', lambda e: e.copy(tb[:, b, w, hh * 4:(hh + 1) * 4, :], pb[:, :].rearrange("p (q c) -> p q c", q=4)),
                             reads=['B%d' % (w * 2 + hh)], writes=['tb%d%d' % (b, w)])
                    else:
                        S.op('dve', lambda e: e.tensor_copy(tb[:, b, w, hh * 4:(hh + 1) * 4, :], pb[:, :].rearrange("p (q c) -> p q c", q=4)),
                             reads=['B%d' % (w * 2 + hh)], writes=['tb%d%d' % (b, w)])
                if w == 0:
                    lt = t - nct
                    S.dma('sp', QT[:, :, lt * 128:(lt + 1) * 128], tb[:, b, 0], reads=['tb%d0' % b], writes=['QT'])
                else:
                    S.dma('sp', KT[:, :, t * 128:(t + 1) * 128], tb[:, b, 1], reads=['tb%d1' % b], writes=['KT'])
    QG = min(512, TL)
    nqs = QG // 128
    nqg = TL // QG
    with ExitStack() as ph:
        S.barrier()
        lamt = sbt(nc, ph, 'lamt', [128, 256], F32)
        lamp = sbt(nc, ph, 'lamp', [128, 2, 64], F32)
        lams = sbt(nc, ph, 'lams', [128, 4], F32)
        ng = sbt(nc, ph, 'ng', [128, 128], F32)
        epsr = sbt(nc, ph, 'epsr', [128, 1], F32)
        S.op('pool', lambda e: e.memset(epsr[:], 1e-5), writes=['epsr'])
        S.dma('sp', lamt[:], I['da_lambda'].rearrange("a b -> (a b)").partition_broadcast(128), writes=['lamt'])
        S.dma('sp', ng[:], I['da_norm_g'].partition_broadcast(128), writes=['ng'])
        lv = lamt[:].rearrange("p (a b) -> p a b", a=4)
        S.op('dve', lambda e: e.tensor_tensor(lamp[:, 0, :], lv[:, 0, :], lv[:, 1, :], ALU.mult), reads=['lamt'], writes=['lamp'])
        S.op('dve', lambda e: e.tensor_tensor(lamp[:, 1, :], lv[:, 2, :], lv[:, 3, :], ALU.mult), reads=['lamp', 'lamt'], writes=['lamp'])
        S.op('dve', lambda e: e.reduce_sum(lams[:, 0:2], lamp[:], AX.X), reads=['lamp'], writes=['lams'])
        S.op('act', lambda e: e.activation(lams[:, 0:2], lams[:, 0:2], AF.Exp), reads=['lams'], writes=['lams'])
        S.op('dve', lambda e: e.tensor_tensor(lams[:, 2:3], lams[:, 0:1], lams[:, 1:2], ALU.subtract), reads=['lams'], writes=['lams'])
        S.op('dve', lambda e: e.tensor_scalar(lams[:, 3:4], lams[:, 2:3], float(lam_init), -1.0, ALU.add, ALU.mult), reads=['lams'], writes=['lams'])
        S.op('pool', lambda e: e.tensor_scalar_mul(ng[:], ng[:], float(1.0 - lam_init)), reads=['ng'], writes=['ng'])
        qTh = sbt(nc, ph, 'qTh', [128, 2, TL], BF16)
        kTh = sbt(nc, ph, 'kTh', [128, 2, NT], BF16)
        vh = sbt(nc, ph, 'vh', [128, 2, ntile, 132], BF16)
        vall = sbt(nc, ph, 'vall', [128, ntile, 1024], BF16)
        PT = sbt(nc, ph, 'PT', [128, 2, QG], BF16)
        o0 = sbt(nc, ph, 'o0', [128, 4, 128], F32)
        rr = sbt(nc, ph, 'rr', [128, 2, 4], F32)
        oo = sbt(nc, ph, 'oo', [128, 2, 128], F32)
        sq = sbt(nc, ph, 'sq', [128, 128], F32)
        ss = sbt(nc, ph, 'ss', [128, 2], F32)
        vsrc = QKV.rearrange("(n p) c -> p n c", p=128)
        S.dma('pool', vall[:], vsrc[:, :, 2048:3072], reads=[qkey], writes=['vall'])
        for hb_ in range(2):
            S.op('pool', lambda e: e.memset(vh[:, hb_, :, 128:129], 1.0), writes=['vh%d' % hb_])
        nst = 0
        for h in range(8):
            hb = h % 2
            S.dma('sp', qTh[:, hb, :], QT[:, h, :], reads=['QT'], writes=['qTh%d' % hb])
            S.dma('sp', kTh[:, hb, :], KT[:, h, :], reads=['KT'], writes=['kTh%d' % hb])
            S.op('act', lambda e: e.copy(vh[:, hb, :, 0:128], vall[:, :, h * 128:(h + 1) * 128]), reads=['vall'], writes=['vh%d' % hb])
            for qg in range(nqg):
                for m in range(2):
                    for s_ in range(ntile):
                        pb_i = nst % 2
                        nst += 1
                        psc = K.PS[pb_i]
                        S.op('pe', lambda e: e.matmul(psc[:, :QG], kTh[m * 64:(m + 1) * 64, hb, s_ * 128:(s_ + 1) * 128],
                                                      qTh[m * 64:(m + 1) * 64, hb, qg * QG:(qg + 1) * QG], start=True, stop=True),
                             reads=['kTh%d' % hb, 'qTh%d' % hb], writes=['B%d' % pb_i])
                        S.op('act', lambda e: e.activation(PT[:, pb_i, :], psc[:, :QG], AF.Exp, scale=0.125),
                             reads=['B%d' % pb_i], writes=['PT%d' % pb_i])
                        for qs in range(nqs):
                            S.op('pe', lambda e: e.matmul(K.PS[4 + qs][:, 0:129], PT[:, pb_i, qs * 128:(qs + 1) * 128], vh[:, hb, s_, 0:129],
                                                          start=(s_ == 0), stop=(s_ == ntile - 1)),
                                 reads=['PT%d' % pb_i, 'vh%d' % hb], writes=['B%d' % (4 + qs)])
                    for qs in range(nqs):
                        acc = K.PS[4 + qs]
                        S.op('dve', lambda e: e.reciprocal(rr[:, m, qs:qs + 1], acc[:, 128:129]), reads=['B%d' % (4 + qs)], writes=['rr'])
                        if m == 0:
                            S.op('dve', lambda e: e.tensor_scalar(o0[:, qs, :], acc[:, 0:128], rr[:, 0, qs:qs + 1], None, ALU.mult),
                                 reads=['B%d' % (4 + qs), 'rr'], writes=['o0'])
                        else:
                            ob = qs % 2
                            S.op('dve', lambda e: e.tensor_scalar(rr[:, 1, qs:qs + 1], rr[:, 1, qs:qs + 1], lams[:, 3:4], None, ALU.mult),
                                 reads=['rr', 'lams'], writes=['rr'])
                            S.op('dve', lambda e: e.scalar_tensor_tensor(oo[:, ob, :], acc[:, 0:128], rr[:, 1, qs:qs + 1], o0[:, qs, :], ALU.mult, ALU.add),
                                 reads=['B%d' % (4 + qs), 'rr', 'o0'], writes=['oo%d' % ob])
                            S.op('act', lambda e: e.activation(sq[:], oo[:, ob, :], AF.Square, accum_out=ss[:, 0:1]), reads=['oo%d' % ob], writes=['sq', 'ss'])
                            S.op('act', lambda e: e.activation(ss[:, 1:2], ss[:, 0:1], AF.Sqrt, bias=epsr[:], scale=1.0 / 128.0), reads=['ss', 'epsr'], writes=['ss'])
                            S.op('dve', lambda e: e.reciprocal(ss[:, 1:2], ss[:, 1:2]), reads=['ss'], writes=['ss'])
                            S.op('dve', lambda e: e.scalar_tensor_tensor(oo[:, ob, :], oo[:, ob, :], ss[:, 1:2], ng[:], ALU.mult, ALU.mult),
                                 reads=['oo%d' % ob, 'ss', 'ng'], writes=['oo%d' % ob])
                            r0 = TC + qg * QG + qs * 128
                            S.dma('sp', Mix[r0:r0 + 128, h * 128:(h + 1) * 128], oo[:, ob, :], reads=['oo%d' % ob], writes=[mkey])


def rope_tables(TL):
    n_rows = TL // 64
    row = np.repeat(np.arange(n_rows), 64).astype(np.float32)
    col = np.tile(np.arange(64), n_rows).astype(np.float32)
    n_freq = 16
    inv = (10000.0 ** (-np.arange(n_freq, dtype=np.float32) / n_freq)).astype(np.float32)
    ar = row[:, None] * inv
    ac = col[:, None] * inv
    ang = np.concatenate([ar, ar, ac, ac], axis=-1).astype(np.float32)
    cos, sin = np.cos(ang).astype(np.float32), np.sin(ang).astype(np.float32)
    sgn = np.tile(np.concatenate([-np.ones(16), np.ones(16)]), 2).astype(np.float32)
    return cos, (sin * sgn[None, :]).astype(np.float32)


def make_masks():
    s_ = np.arange(128)[:, None]
    t_ = np.arange(128)[None, :]
    same = (s_ // 64) == (t_ // 64)
    m = np.stack([s_ <= t_, s_ >= t_, s_ < t_, s_ > t_, (s_ <= t_) & same, (s_ >= t_) & same, same]).astype(np.float32)
    return np.ascontiguousarray(m)


def host_inputs(b, TL, TC, x, c, ctx, c_ctx, **w):
    ins = {}
    ins['h0'] = np.ascontiguousarray(np.concatenate([ctx[b, :TC], x[b, :TL]], axis=0))
    cc = np.stack([c[b], c_ctx], axis=-1)
    ins['cc'] = np.ascontiguousarray(cc.reshape(8, 128, 2).transpose(1, 0, 2))
    for k in ('w_mod', 'b_mod', 'ln_g', 'ln_b', 'ffn_w_in', 'ffn_w_out'):
        ins[k] = np.ascontiguousarray(w[k])
    ins['ident'] = np.eye(128, dtype=np.float32)
    ins['ev_w_in'] = np.ascontiguousarray(w['ev_w_in'][0])
    ins['ev_w_out'] = np.ascontiguousarray(w['ev_w_out'][0])
    ins['od_w_in'] = np.ascontiguousarray(w['od_w_in'][0])
    ins['od_w_out'] = np.ascontiguousarray(w['od_w_out'][0])
    ins['da_lambda'] = np.ascontiguousarray(w['da_lambda'][0])
    ins['da_norm_g'] = np.ascontiguousarray(w['da_norm_g'][0])
    ins['rope_cos'], ins['rope_sin'] = rope_tables(TL)
    ins['masks'] = make_masks()
    ins['hg_lb'] = np.ascontiguousarray(w['hg_lb'])
    for k in ('rw_mu', 'rw_w0', 'rw_a0', 'rw_k_k', 'rw_k_a', 'rw_gn_g', 'rw_gn_b'):
        ins[k] = np.ascontiguousarray(w[k][0])
    w2p = np.zeros((2, 128, 512), np.float32)
    w2p[0, 0:64] = w['rw_w2'][0, 0]
    w2p[1, 64:128] = w['rw_w2'][0, 1]
    ins['rw_w2p'] = w2p
    a2p = np.zeros((128, 512), np.float32)
    a2p[0:64] = w['rw_a2'][0]
    ins['rw_a2p'] = a2p
    ins['rw_g2a'] = np.ascontiguousarray(w['rw_g2'][0][0:128])
    g2b = np.zeros((128, 512), np.float32)
    g2b[0:32] = w['rw_g2'][0][128:160]
    ins['rw_g2b'] = g2b
    p_ = np.arange(128)
    ins['rowmask'] = np.ascontiguousarray(np.stack([p_ != 0, p_ != 127, p_ < 64, p_ >= 64], axis=1).astype(np.float32))
    ins['rw_r_k'] = np.ascontiguousarray(w['rw_r_k'][0].reshape(512))
    ins['hg_norm_g'] = np.ascontiguousarray(w['hg_norm_g'][0])
    return ins


def kernel(**inputs):
    inputs = {k: np.asarray(v) for k, v in inputs.items()}
    B, TL, _ = inputs['x'].shape
    TC = inputs['ctx'].shape[1]
    in_maps = [host_inputs(b, TL, TC, **inputs) for b in range(B)]
    build(TL, TC, layers=(0,), full_out=True)
    ncA = build(TL, TC, layers=(0,), full_out=True, needed=build.last.S.waited)
    resA = run_bass_kernel_spmd(ncA, in_maps, core_ids=list(range(B)))
    in_maps_b = []
    for b in range(B):
        m = dict(in_maps[b])
        m['h0'] = np.ascontiguousarray(np.asarray(resA.results[b]['out'], dtype=np.float32))
        in_maps_b.append(m)
    build(TL, TC, layers=(1,))
    ncB = build(TL, TC, layers=(1,), needed=build.last.S.waited)
    res = run_bass_kernel_spmd(ncB, in_maps_b, core_ids=list(range(B)))
    return np.stack([np.asarray(r['out']) for r in res.results], axis=0).astype(np.float32)
```
